# Optimizing a Trainium2 kernel written in Bass

```python
import jax, jax.numpy as jnp
from jax import lax
import numpy as np

D_MODEL = 1024
BATCH = 2
SEQ = 8192
DEPTH = 1
DEC_BATCH = 8
DEC_SEQ = 32
PAST_LEN = 2048

CHUNK = 64
MIX_WIDTH = D_MODEL
CONV_DIM = MIX_WIDTH // 2
RWKV_DIM = MIX_WIDTH - CONV_DIM
HEAD_SIZE = 64
N_HEADS = RWKV_DIM // HEAD_SIZE
CONV_K = 3
DECAY_LORA = 64
AAA_LORA = 64
GATE_LORA = 128
RWKV_PROJ = 3 * RWKV_DIM + DECAY_LORA + AAA_LORA + GATE_LORA
PROJ_DIM = 3 * CONV_DIM + RWKV_PROJ
D_FF = 4 * D_MODEL
NORM_EPS = 1e-6
GN_EPS = 64e-5
DECAY_SCALE = float(np.exp(-0.5))

kernel_name = "hymba_conv_rwkv7_stream_step"


def _rmsnorm(x, g):
    xf = x.astype(jnp.float32)
    y = xf * lax.rsqrt(jnp.mean(xf * xf, axis=-1, keepdims=True) + NORM_EPS)
    return (y * g.astype(jnp.float32)).astype(x.dtype)


def _causal_conv(u, buf, w):
    T = u.shape[1]
    up = jnp.concatenate([buf.astype(u.dtype), u], axis=1)
    y = up[:, 0:T] * w[0]
    for j in range(1, CONV_K):
        y = y + up[:, j:j + T] * w[j]
    return y, up[:, T:]


def _wkv7(r, w, k, v, kk, a, s0):
    def step(S, inp):
        r_t, w_t, k_t, v_t, kk_t, a_t = inp
        sa = jnp.einsum('bhvk,bhk->bhv', S, -kk_t)
        S = (S * w_t[:, :, None, :]
             + sa[..., None] * (kk_t * a_t)[:, :, None, :]
             + v_t[..., None] * k_t[:, :, None, :])
        return S, jnp.einsum('bhvk,bhk->bhv', S, r_t)
    xs = tuple(jnp.swapaxes(t, 0, 1) for t in (r, w, k, v, kk, a))
    S, o = lax.scan(step, s0, xs)
    return jnp.swapaxes(o, 0, 1), S


def _mixer(h, conv_buf, shift_prev, wkv_state, w_in, conv_w, shift_mu, w_decay2, decay_w0,
           w_a2, a0, w_g2, k_k, k_a, r_k, gn_gain, gn_bias, w_out):
    B, T, _ = h.shape
    z = h @ w_in
    zb, zc, zh, zr = jnp.split(z, [CONV_DIM, 2 * CONV_DIM, 3 * CONV_DIM], axis=-1)

    yc, new_conv = _causal_conv(zc * zh, conv_buf, conv_w)
    y_conv = zb * yc

    zprev = jnp.concatenate([shift_prev[:, None].astype(zr.dtype), zr[:, :-1]], axis=1)
    zs = (zr + shift_mu * (zprev - zr)).astype(jnp.float32)
    new_shift = zr[:, -1]
    r, k, v, lw, la, lg = jnp.split(
        zs, [RWKV_DIM, 2 * RWKV_DIM, 3 * RWKV_DIM, 3 * RWKV_DIM + DECAY_LORA,
             3 * RWKV_DIM + DECAY_LORA + AAA_LORA], axis=-1)
    f32 = lambda t: t.astype(jnp.float32)
    logit_w = f32(decay_w0) + jnp.tanh(lw) @ f32(w_decay2)
    w = jnp.exp(-DECAY_SCALE * jax.nn.sigmoid(logit_w))
    a = jax.nn.sigmoid(f32(a0) + la @ f32(w_a2))
    g = jax.nn.sigmoid(lg) @ f32(w_g2)

    hs = lambda t: t.reshape(B, T, N_HEADS, HEAD_SIZE)
    hp = lambda t: f32(t).reshape(N_HEADS, HEAD_SIZE)
    r, k, v, w, a = hs(r), hs(k), hs(v), hs(w), hs(a)
    kk = k * hp(k_k)
    kk = kk / jnp.maximum(jnp.sqrt(jnp.sum(kk * kk, axis=-1, keepdims=True)), 1e-12)
    k = k * (1.0 + (a - 1.0) * hp(k_a))
    o, S = _wkv7(r, w, k, v, kk, a, wkv_state.astype(jnp.float32))

    mu = jnp.mean(o, axis=-1, keepdims=True)
    var = jnp.mean(jnp.square(o - mu), axis=-1, keepdims=True)
    o = (o - mu) * lax.rsqrt(var + GN_EPS)
    o = o * hp(gn_gain) + hp(gn_bias)
    o = o + jnp.sum(r * k * f32(r_k), axis=-1, keepdims=True) * v
    y_rwkv = (o.reshape(B, T, RWKV_DIM) * g).astype(h.dtype)

    y = jnp.concatenate([y_conv.astype(h.dtype), y_rwkv], axis=-1) @ w_out
    return y, new_conv, new_shift, S.astype(h.dtype)


def _layer(x, conv_buf, shift_prev, wkv_state, p):
    (norm_mix_pre, norm_mix_post, norm_ffn_pre, norm_ffn_post, w_in, conv_w, shift_mu,
     w_decay2, decay_w0, w_a2, a0, w_g2, k_k, k_a, r_k, gn_gain, gn_bias, w_out,
     w_ff1, w_ff2) = p
    h = _rmsnorm(x, norm_mix_pre)
    m, new_conv, new_shift, new_wkv = _mixer(
        h, conv_buf, shift_prev, wkv_state, w_in, conv_w, shift_mu, w_decay2, decay_w0,
        w_a2, a0, w_g2, k_k, k_a, r_k, gn_gain, gn_bias, w_out)
    x = x + _rmsnorm(m, norm_mix_post)
    h2 = _rmsnorm(x, norm_ffn_pre)
    f = jnp.square(jax.nn.relu(h2 @ w_ff1)) @ w_ff2
    x = x + _rmsnorm(f, norm_ffn_post)
    return x, new_conv, new_shift, new_wkv


def setup_inputs(seed: int = 0) -> dict:
    key = jax.random.key(seed)
    ks = jax.random.split(key, 32)
    nrm = lambda k, shape, s: jax.random.normal(k, shape, jnp.float32) * s
    L = DEPTH
    return {
        "x_prompt": nrm(ks[0], (BATCH, SEQ, D_MODEL), 1.0),
        "x_sample": nrm(ks[1], (DEC_BATCH, DEC_SEQ, D_MODEL), 1.0),
        "state_conv": nrm(ks[2], (L, DEC_BATCH, CONV_K - 1, CONV_DIM), 1.0),
        "state_shift": nrm(ks[3], (L, DEC_BATCH, RWKV_PROJ), 1.0),
        "state_wkv": nrm(ks[4], (L, DEC_BATCH, N_HEADS, HEAD_SIZE, HEAD_SIZE), 0.5),
        "norm_mix_pre": 1.0 + nrm(ks[5], (L, D_MODEL), 0.05),
        "norm_mix_post": 1.0 + nrm(ks[6], (L, D_MODEL), 0.05),
        "norm_ffn_pre": 1.0 + nrm(ks[7], (L, D_MODEL), 0.05),
        "norm_ffn_post": 1.0 + nrm(ks[8], (L, D_MODEL), 0.05),
        "w_in": nrm(ks[9], (L, D_MODEL, PROJ_DIM), D_MODEL ** -0.5),
        "conv_w": nrm(ks[10], (L, CONV_K, CONV_DIM), CONV_K ** -0.5),
        "shift_mu": jax.random.uniform(ks[11], (L, RWKV_PROJ), jnp.float32),
        "w_decay2": nrm(ks[12], (L, DECAY_LORA, RWKV_DIM), DECAY_LORA ** -0.5),
        "decay_w0": nrm(ks[13], (L, RWKV_DIM), 0.5),
        "w_a2": nrm(ks[14], (L, AAA_LORA, RWKV_DIM), AAA_LORA ** -0.5),
        "a0": nrm(ks[15], (L, RWKV_DIM), 0.1),
        "w_g2": nrm(ks[16], (L, GATE_LORA, RWKV_DIM), GATE_LORA ** -0.5),
        "k_k": 0.85 + nrm(ks[17], (L, RWKV_DIM), 0.05),
        "k_a": 1.0 + nrm(ks[18], (L, RWKV_DIM), 0.05),
        "r_k": nrm(ks[19], (L, N_HEADS, HEAD_SIZE), 0.1),
        "gn_gain": 1.0 + nrm(ks[20], (L, RWKV_DIM), 0.05),
        "gn_bias": nrm(ks[21], (L, RWKV_DIM), 0.02),
        "w_out": nrm(ks[22], (L, MIX_WIDTH, D_MODEL), MIX_WIDTH ** -0.5),
        "w_ff1": nrm(ks[23], (L, D_MODEL, D_FF), D_MODEL ** -0.5),
        "w_ff2": nrm(ks[24], (L, D_FF, D_MODEL), D_FF ** -0.5),
    }


def reference(x_prompt, x_sample, state_conv, state_shift, state_wkv,
              norm_mix_pre, norm_mix_post, norm_ffn_pre, norm_ffn_post, w_in, conv_w,
              shift_mu, w_decay2, decay_w0, w_a2, a0, w_g2, k_k, k_a, r_k, gn_gain,
              gn_bias, w_out, w_ff1, w_ff2):
    Bp = x_prompt.shape[0]
    dt = x_prompt.dtype
    xp, xs = x_prompt, x_sample
    conv_p, shift_p, wkv_p, conv_s, shift_s, wkv_s = [], [], [], [], [], []
    for l in range(DEPTH):
        p = (norm_mix_pre[l], norm_mix_post[l], norm_ffn_pre[l], norm_ffn_post[l], w_in[l],
             conv_w[l], shift_mu[l], w_decay2[l], decay_w0[l], w_a2[l], a0[l], w_g2[l],
             k_k[l], k_a[l], r_k[l], gn_gain[l], gn_bias[l], w_out[l], w_ff1[l], w_ff2[l])
        xp, c_p, s_p, w_p = _layer(
            xp,
            jnp.zeros((Bp, CONV_K - 1, CONV_DIM), dt),
            jnp.zeros((Bp, RWKV_PROJ), dt),
            jnp.zeros((Bp, N_HEADS, HEAD_SIZE, HEAD_SIZE), dt),
            p)
        xs, c_s, s_s, w_s = _layer(xs, state_conv[l], state_shift[l], state_wkv[l], p)
        conv_p.append(c_p); shift_p.append(s_p); wkv_p.append(w_p)
        conv_s.append(c_s); shift_s.append(s_s); wkv_s.append(w_s)
    conv_prompt = jnp.stack(conv_p)
    shift_prompt = jnp.stack(shift_p)
    wkv_prompt = jnp.stack(wkv_p)
    conv_sample = jnp.stack(conv_s)
    shift_sample = jnp.stack(shift_s)
    wkv_sample = jnp.stack(wkv_s)
    return (xp, xs, conv_prompt, shift_prompt, wkv_prompt, conv_sample, shift_sample, wkv_sample)
```

```python
from contextlib import ExitStack
import numpy as np
import ml_dtypes
import concourse.bass as bass
import concourse.mybir as mybir
from concourse.bass_utils import run_bass_kernel_spmd

F32 = mybir.dt.float32
BF16 = mybir.dt.bfloat16
I32 = mybir.dt.int32
AF = mybir.ActivationFunctionType
ALU = mybir.AluOpType

D = 1024
SEQ = 8192
SEG = 2048
TS = 32
NPROJ = 3328
DFF = 4096
NEPS = 1e-6
GEPS = 64e-5
DSC = float(np.exp(-0.5))
TG = 256
CH = 128


class Buf:
    def __init__(self, name, multi=False):
        self.name = name
        self.w = []
        self.r = []
        self.multi = multi
        self.sem = None
        self.cnt = 0
        self.bank = None


class Op:
    __slots__ = ("eng", "fn", "deps", "bdeps", "dma", "sig", "need", "idx", "cc")


class Prog:
    def __init__(self, nc, es):
        self.nc = nc
        self.es = es
        self.ops = []
        self.last = {}
        self.nsem = 0
        self.bank_last = {}
        self.cap = None

    def newsem(self, name):
        self.nsem += 1
        return self.es.enter_context(self.nc.semaphore(f"{name}_{self.nsem}"))

    def merge(self, A, B):
        ia = ib = 0
        na, nb = len(A), len(B)
        while ia < na or ib < nb:
            if ib >= nb or (ia < na and ia * nb * 10 <= ib * na * 6):
                self.add(*A[ia])
                ia += 1
            else:
                self.add(*B[ib])
                ib += 1

    def add(self, eng, fn, R=(), W=(), dma=None, cc=False):
        if self.cap is not None:
            self.cap.append((eng, fn, list(R), list(W), dma, cc))
            return -1
        o = Op()
        o.cc = cc
        o.eng, o.fn, o.dma, o.sig, o.need = eng, fn, dma, None, False
        o.idx = len(self.ops)
        deps = set()
        for b in R:
            deps.update(b.w)
        for b in W:
            deps.update(b.r)
            if not b.multi:
                deps.update(b.w)
        for b in R:
            b.r.append(o.idx)
        for b in W:
            if b.multi:
                b.w.append(o.idx)
            else:
                b.w = [o.idx]
                b.r = []
        o.deps = deps
        o.bdeps = set()
        for b in list(R) + list(W):
            if b.bank is not None:
                if b.bank in self.bank_last:
                    o.bdeps.add(self.bank_last[b.bank])
        for b in list(R) + list(W):
            if b.bank is not None:
                self.bank_last[b.bank] = o.idx
        self.ops.append(o)
        self.last[eng] = o.idx
        return o.idx

    def barrier(self):
        lasts = set(self.last.values())
        for eng in ("pe", "act", "dve", "pool", "sp"):
            i = self.add(eng, lambda e: e.nop())
            self.ops[i].deps = set(lasts)
        lasts = set(self.last.values())
        for eng in ("pe", "act", "dve", "pool", "sp"):
            i = self.add(eng, lambda e: e.nop())
            self.ops[i].deps = set(lasts)

    def finalize(self, final_dma_ops):
        ops = self.ops
        fin = self.add("sp", lambda e: e.nop())
        ops[fin].deps = set(final_dma_ops)
        for o in ops:
            if o.dma is not None:
                o.need = True
        for o in ops:
            keep = set()
            for d in o.deps:
                p = ops[d]
                if p.eng == o.eng and p.dma is None and o.dma is None and o.eng == "pe":
                    continue
                keep.add(d)
                p.need = True
            for d in o.bdeps:
                p = ops[d]
                if p.eng != o.eng:
                    keep.add(d)
                    p.need = True
            o.deps = keep
        esem = {e: self.newsem("e" + e) for e in ("pe", "act", "dve", "pool", "sp")}
        ecnt = {e: 0 for e in esem}
        for o in ops:
            if not o.need:
                continue
            if o.dma is not None:
                b = o.dma
                if b.sem is None:
                    b.sem = self.newsem("d")
                if o.cc:
                    b.cnt += 1
                    o.sig = (b.sem, b.cnt, None)
                else:
                    b.cnt += 16
                    o.sig = (b.sem, b.cnt, 16)
            else:
                ecnt[o.eng] += 1
                o.sig = (esem[o.eng], ecnt[o.eng], 1)
        import os as _o2
        if _o2.environ.get("KDBG_PRINT"):
            print("SEMCOUNTS", ecnt, "nops", len(ops), {e: sum(1 for o in ops if o.eng == e) for e in ecnt}, flush=True)

    def emit(self, block):
        ops = self.ops
        streams = {e: [o for o in ops if o.eng == e] for e in ("pe", "act", "dve", "pool", "sp")}

        def run(engobj, lst):
            waited = {}
            for o in lst:
                need = {}
                for d in o.deps:
                    s, v, _ = ops[d].sig
                    k = id(s)
                    if waited.get(k, 0) < v and need.get(k, (None, 0))[1] < v:
                        need[k] = (s, v)
                for k, (s, v) in need.items():
                    engobj.wait_ge(s, v)
                    waited[k] = v
                ins = o.fn(engobj)
                if o.sig is not None:
                    if o.sig[2] is None:
                        ins.then_inc(o.sig[0])
                    else:
                        ins.then_inc(o.sig[0], o.sig[2])

        @block.tensor
        def _(e):
            run(e, streams["pe"])

        @block.scalar
        def _(e):
            run(e, streams["act"])

        @block.vector
        def _(e):
            run(e, streams["dve"])

        @block.gpsimd
        def _(e):
            run(e, streams["pool"])

        @block.sync
        def _(e):
            run(e, streams["sp"])


def build_program():
    nc = bass.Bass("TRN2", target_bir_lowering=False)
    es = ExitStack()
    P = Prog(nc, es)

    def din(name, shape, dt=F32):
        return nc.dram_tensor(name, list(shape), dt, kind="ExternalInput").ap()

    def dout(name, shape, dt=F32):
        return nc.dram_tensor(name, list(shape), dt, kind="ExternalOutput").ap()

    xpre = din("xpre", [3 * SEG, D])
    xseg = din("xseg", [SEG, D])
    xhalo = din("xhalo", [2, D])
    xs_in = din("xs", [TS, D])
    sconv = din("sconv", [128, 4, 2])
    sshift = din("sshift", [128, 14])
    swkv = din("swkv", [128, 4, 64])
    w_in = din("w_in", [D, NPROJ])
    w_out = din("w_out", [D, D])
    w_ff1 = din("w_ff1", [D, DFF])
    w_ff2 = din("w_ff2", [DFF, D])
    wda_full = din("wda_full", [128, 512])
    wg_full = din("wg_full", [128, 512])
    par_full = din("par_full", [128, 4, 10])
    mu_sh = din("mu_sh", [128, 2])
    convw = din("convw", [128, 4, 3])
    gpre = din("gpre", [128, 16])
    gpost_d = din("gpost", [128, D])
    gffn_d = din("gffn", [128, D])
    cmask = din("cmask", [128, 7, 128])

    y_seg = dout("y_seg", [SEG, D])
    y_s = dout("y_s", [TS, D])
    ucv_p = dout("ucv_p", [128, 4, 2])
    ucv_s = dout("ucv_s", [128, 4, 2])
    zl_p = dout("zl_p", [128, 4, 5])
    zl_s = dout("zl_s", [128, 4, 5])
    H_p = dout("H_p", [128, 4, 64])
    H_s = dout("H_s", [128, 4, 64])

    x1scr = nc.dram_tensor("x1scr", [SEG + TS, D], F32).ap()
    x1scr_b = Buf("x1scr", multi=True)

    out_dma_ops = []
    import os as _os
    KRSTOP = int(_os.environ.get("KDBG_RSTOP", "99"))

    class Stop(Exception):
        pass

    def rchk(n):
        if KRSTOP == n:
            raise Stop()

    def sb(stack, name, shape, dt=F32):
        t = stack.enter_context(nc.sbuf_tensor(name, list(shape), dt))
        return t, Buf(name)

    def pst(name, shape, dt=F32):
        t = es.enter_context(nc.psum_tensor(name, list(shape), dt))
        return t

    def mm(out, lhsT, rhs, R, W, start=True, stop=True):
        P.add("pe", lambda e: e.matmul(out, lhsT=lhsT, rhs=rhs, start=start, stop=stop), R, W)

    def tr(out, in_, ident, R, W):
        P.add("pe", lambda e: e.transpose(out, in_, ident), R, W)

    def act(out, in_, func, R, W, bias=None, scale=None, accum=None):
        kw = {}
        if bias is not None:
            kw["bias"] = bias
        if scale is not None:
            kw["scale"] = scale
        if accum is not None:
            kw["accum_out"] = accum
        P.add("act", lambda e: e.activation(out=out, in_=in_, func=func, **kw), R, W)

    def ts(out, in0, s1, s2, op0, op1, R, W, eng="dve"):
        if op1 is None:
            P.add(eng, lambda e: e.tensor_scalar(out=out, in0=in0, scalar1=s1, scalar2=None, op0=op0), R, W)
        else:
            P.add(eng, lambda e: e.tensor_scalar(out=out, in0=in0, scalar1=s1, scalar2=s2, op0=op0, op1=op1), R, W)

    def tt(out, in0, in1, op, R, W, eng="dve"):
        P.add(eng, lambda e: e.tensor_tensor(out=out, in0=in0, in1=in1, op=op), R, W)

    def stt(out, in0, scalar, in1, op0, op1, R, W):
        P.add("dve", lambda e: e.scalar_tensor_tensor(out=out, in0=in0, scalar=scalar, in1=in1, op0=op0, op1=op1), R, W)

    def cp(out, in_, R, W, eng="dve"):
        if eng == "act":
            act(out, in_, AF.Copy, R, W)
        else:
            P.add(eng, lambda e: e.tensor_copy(out=out, in_=in_), R, W)

    def recip(out, in_, R, W):
        P.add("dve", lambda e: e.reciprocal(out=out, in_=in_), R, W)

    def dma(out, in_, R, W, prim, q="sp"):
        return P.add(q, lambda e: e.dma_start(out=out, in_=in_), R, W, dma=prim)

    cm, cm_b = sb(es, "cm", [128, 7, 128])
    cmb, cmb_b = sb(es, "cmb", [128, 128], BF16)
    gpre_t, gpre_b = sb(es, "gpre_t", [128, 16])
    gpost, gpost_b = sb(es, "gpost_t", [128, D])
    gffn, gffn_b = sb(es, "gffn_t", [128, D])
    parf, parf_b = sb(es, "parf", [128, 4, 12])
    mush, mush_b = sb(es, "mush", [128, 2])
    cvw, cvw_b = sb(es, "cvw", [128, 4, 3])
    mu5all, mu5_b = sb(es, "mu5all", [128, 4, 5])
    xt = [sb(es, f"xt{i}", [128, D]) for i in range(2)]
    xn, xn_b = sb(es, "xn", [128, D], BF16)
    junk, junk_b = sb(es, "junk", [128, D], BF16)
    hT, hT_b = sb(es, "hT", [128, 8, 512], BF16)
    t1k, t1k_b = sb(es, "t1k", [128, D])
    x1t, x1t_b = sb(es, "x1t", [128, D])
    sm, sm_b = sb(es, "sm", [128, 8])
    sm2, sm2_b = sb(es, "sm2", [128, 8])

    IDF = cm[:, 0, :]
    MUS = cm[:, 1, :]
    MUI = cm[:, 2, :]
    MUSN = cm[:, 3, :]
    MLS = cm[:, 4, :]
    BONES = cm[:, 5, :]
    ONES = cm[:, 6, :]

    ps_tr = pst("ps_tr", [128, 1024], BF16)
    ps_tr_b = Buf("ps_tr")
    ps_tr_b.bank = 0
    ps_tfb = pst("ps_tfb", [128, 512])
    ps_tf = ps_tfb[:, 0:256]
    ps_tf_b = Buf("ps_tf")
    ps_tf_b.bank = 1
    ps_t2 = ps_tfb[:, 256:512].bitcast(BF16)
    ps_t2_b = Buf("ps_t2")
    ps_t2_b.bank = 1
    ps_z = [pst(f"ps_z{i}", [128, 512]) for i in range(2)]
    ps_z_b = [Buf(f"ps_z{i}") for i in range(2)]
    ps_c = [pst(f"ps_c{i}", [128, 512]) for i in range(4)]
    ps_c_b = [[Buf(f"ps_c{i}_{j}") for j in range(4)] for i in range(4)]
    for i in range(2):
        ps_z_b[i].bank = 2 + i
    for i in range(4):
        for j in range(4):
            ps_c_b[i][j].bank = 4 + i
    zrot = [0]

    def next_z():
        zrot[0] ^= 1
        return ps_z[zrot[0]], ps_z_b[zrot[0]]

    dma(cm[:], cmask, [], [cm_b], cm_b)
    cp(cmb[:], cm[:, 0, :], [cm_b], [cmb_b])
    dma(gpre_t[:], gpre, [], [gpre_b], gpre_b)
    dma(gpost[:], gpost_d, [], [gpost_b], gpost_b)
    dma(gffn[:], gffn_d, [], [gffn_b], gffn_b)
    dma(parf[:, :, 0:10], par_full, [], [parf_b], parf_b)
    dma(mush[:], mu_sh, [], [mush_b], mush_b)
    dma(cvw[:], convw, [], [cvw_b], cvw_b)
    for q in range(4):
        ts(parf[:, q, 10:12], parf[:, q, 3:5], -1.0, None, ALU.mult, None, [parf_b], [parf_b])
        cp(mu5all[:, q, 0:3], parf[:, q, 0:3], [parf_b], [mu5_b])
        cp(mu5all[:, q, 3:5], mush[:, 0:2], [mush_b], [mu5_b])

    esA = ExitStack()
    stg = [sb(esA, f"stg{i}", [128, 1024]) for i in range(2)]
    srot = [0]

    def load_cast(dst_ap, dst_b, src_ap, ncols, scale_ap=None, scale_b=None, stg_=None):
        st_ = stg_ if stg_ is not None else stg
        srot[0] = (srot[0] + 1) % len(st_)
        s_t, s_b = st_[srot[0]]
        dma(s_t[:, 0:ncols], src_ap, [], [s_b], s_b)
        if scale_ap is None:
            if srot[0] % 2:
                cp(dst_ap, s_t[:, 0:ncols], [s_b], [dst_b], eng="act")
            else:
                cp(dst_ap, s_t[:, 0:ncols], [s_b], [dst_b])
        else:
            if srot[0] % 2:
                act(dst_ap, s_t[:, 0:ncols], AF.Copy, [s_b, scale_b], [dst_b], scale=scale_ap)
            else:
                ts(dst_ap, s_t[:, 0:ncols], scale_ap, None, ALU.mult, None, [s_b, scale_b], [dst_b])

    wcv, wcv_b = sb(esA, "wcv", [128, 8, 1536], BF16)
    wo, wo_b = sb(esA, "wo", [128, 8, D], BF16)
    wdaf, wdaf_b = sb(esA, "wdaf", [128, 512], BF16)
    wgf, wgf_b = sb(esA, "wgf", [128, 512], BF16)
    TGM = TG
    zr, zr_b = sb(esA, "zr", [128, 5, TGM + 1])
    zs, zs_b = sb(esA, "zs", [128, 5, TGM])
    W_ = {}
    for nm in ("tmp", "tmp2", "sgw", "aa", "kk", "kkn", "ka", "k2", "pp", "Gs", "enG", "eGx", "yn"):
        W_[nm] = sb(esA, "w_" + nm, [128, TGM])
    E_OUT = []
    for s_ in range(2):
        E_OUT.append(dict(
            QR=sb(esA, f"eo{s_}_QR", [128, TGM // CH if TGM >= CH else 1, 2 * CH], BF16),
            PTt=sb(esA, f"eo{s_}_PT", [128, TGM], BF16), KTt=sb(esA, f"eo{s_}_KT", [128, TGM], BF16),
            VTt=sb(esA, f"eo{s_}_VT", [128, TGM], BF16), eG=sb(esA, f"eo{s_}_eG", [128, TGM]),
            bv=sb(esA, f"eo{s_}_bv", [128, TGM]), gate=sb(esA, f"eo{s_}_gate", [128, TGM])))
    lwa, lwa_b = sb(esA, "lwa", [128, TGM], BF16)
    sg, sg_b = sb(esA, "sg", [128, TGM], BF16)
    RfT, RfT_b = sb(esA, "RfT", [128, TGM], BF16)
    NCI = max(1, TG // CH)
    TOKs = [sb(esA, f"TOK{i}", [128, 4, 128], BF16) for i in range(NCI)]
    ONs = [sb(esA, f"ON{i}", [128, 128]) for i in range(NCI)]
    MTps = [sb(esA, f"MTp{i}", [128, 128], BF16) for i in range(NCI)]
    RfT_cb = [Buf(f"RfT{i}") for i in range(NCI)]
    Hbf4, _ = sb(esA, "Hbf4", [128, 4, 64], BF16)
    Hf4, _ = sb(esA, "Hf4", [128, 4, 64])
    Hbf4_b = [Buf(f"Hbf4_{q}") for q in range(4)]
    Hf4_b = [Buf(f"Hf4_{q}") for q in range(4)]
    zcar4 = [sb(esA, f"zcar{q}", [128, 5, 1]) for q in range(4)]
    Hs4, Hs4_b = sb(esA, "Hs4", [128, 4, 64])
    chn = []
    for ci_ in range(NCI):
        row = []
        for h in range(2):
            c_ = {}
            for nm in ("AqpT", "ArpT", "AqkTn", "ArkT", "A", "QYa", "QYb"):
                c_[nm] = sb(esA, f"c{ci_}{h}_{nm}", [128, 128], BF16)
            for nm in ("Pa", "Pb"):
                t_, b_ = sb(esA, f"c{ci_}{h}_{nm}", [128, 256], BF16)
                c_[nm] = (t_[:, 0:128], b_)
                c_[nm + "T"] = (t_[:, 128:256], b_)
                c_["PP" + nm[1]] = (t_, b_)
            c_["st"] = sb(esA, f"c{ci_}{h}_st", [128, 8])
            c_["mv"] = sb(esA, f"c{ci_}{h}_mv", [128, 4])
            row.append(c_)
        chn.append(row)
    for i_ in range(NCI):
        P.add("dve", lambda e, i_=i_: e.memset(MTps[i_][0][:], 0.0), [], [MTps[i_][1]])

    ucv, ucv_b = sb(esA, "ucv", [128, 4, 514])
    zhs, zhs_b = sb(esA, "zhs", [128, 512])
    yct, yct_b = sb(esA, "yct", [128, 512])
    ycv, ycv_b = sb(esA, "ycv", [128, 4, 512], BF16)
    ysm, ysm_b = sb(esA, "ysm", [128, 4, TS], BF16)
    ygt, ygt_b = sb(esA, "ygt", [128, 4, SEG], BF16)

    esS = ExitStack()
    wrw, wrw_b = sb(esS, "wrw", [128, 8, 1792], BF16)

    wcv_kb = [Buf(f"wcv{k}") for k in range(8)]
    wrw_kb = [Buf(f"wrw{k}") for k in range(8)]
    wo_kb = [Buf(f"wo{k}") for k in range(8)]
    wrw_kb2 = [Buf(f"wrwb{k}") for k in range(8)]
    wcv_kb2 = [Buf(f"wcvb{k}") for k in range(8)]
    for kc in range(8):
        load_cast(wrw[:, kc, 0:896], wrw_kb[kc], w_in[kc * 128:(kc + 1) * 128, 1536:1536 + 896], 896,
                  gpre_t[:, kc:kc + 1], gpre_b)
        load_cast(wrw[:, kc, 896:1792], wrw_kb2[kc], w_in[kc * 128:(kc + 1) * 128, 1536 + 896:3328], 896,
                  gpre_t[:, kc:kc + 1], gpre_b)
    for kc in range(8):
        load_cast(wcv[:, kc, 0:768], wcv_kb[kc], w_in[kc * 128:(kc + 1) * 128, 0:768], 768,
                  gpre_t[:, kc:kc + 1], gpre_b)
        load_cast(wcv[:, kc, 768:1536], wcv_kb2[kc], w_in[kc * 128:(kc + 1) * 128, 768:1536], 768,
                  gpre_t[:, kc:kc + 1], gpre_b)
    for kc in range(8):
        load_cast(wo[:, kc, :], wo_kb[kc], w_out[kc * 128:(kc + 1) * 128, :], D)
    load_cast(wdaf[:], wdaf_b, wda_full, 512)
    load_cast(wgf[:], wgf_b, wg_full, 512)

    def rstd_from(acc_ap, n_part, scale, eps, R_b):
        ts(sm[:n_part, 6:7], acc_ap, scale, eps, ALU.mult, ALU.add, [R_b], [sm_b])
        act(sm[:n_part, 6:7], sm[:n_part, 6:7], AF.Ln, [sm_b], [sm_b])
        act(sm[:n_part, 7:8], sm[:n_part, 6:7], AF.Exp, [sm_b], [sm_b], scale=-0.5)
        return sm[:n_part, 7:8]

    def norm_transpose(x_ap_fn, ntiles, tn, dstT, dstT_b):
        for ti in range(ntiles):
            x_t, x_b = xt[ti % 2]
            dma(x_t[:tn, :], x_ap_fn(ti), [], [x_b], x_b)
            act(junk[:tn, :], x_t[:tn, :], AF.Square, [x_b], [junk_b, sm_b], accum=sm[:tn, 0:1])
            r = rstd_from(sm[:tn, 0:1], tn, 1.0 / D, NEPS, sm_b)
            act(xn[:tn, :], x_t[:tn, :], AF.Copy, [x_b, sm_b], [xn_b], scale=r)
            for kc in range(8):
                tr(ps_tr[:, kc * 128:kc * 128 + tn], xn[:tn, kc * 128:(kc + 1) * 128], cmb[:tn, :tn],
                   [xn_b, cmb_b], [ps_tr_b])
            src = ps_tr[:, :].rearrange("p (k t) -> p k t", k=8)[:, :, 0:tn]
            cp(dstT[:, :, ti * 128:ti * 128 + tn], src, [ps_tr_b], [dstT_b], eng=("act" if ti % 2 else "dve"))

    def rwkv_pair(T, wcol, wcol_b, par, par_b, wda_ap, wda_b, wg_ap, wg_b, init_prev, H_init, y_dst, y_dst_b,
                  first, last_out, Hbf=None, Hbf_b=None, Hf=None, Hf_b=None, zcar=None, zcar_b=None, so=False, full_proj=False, eset=0, part="EC", mu5=None):
        C = min(CH, T)
        nch = T // C
        L = int(np.log2(C)) - 1
        tmp, tmp_b = W_["tmp"]
        tmp2, tmp2_b = W_["tmp2"]
        sgw, sgw_b = W_["sgw"]
        aa, aa_b = W_["aa"]
        gate, gate_b = E_OUT[eset]["gate"]
        QR, QR_b = E_OUT[eset]["QR"]
        PTt, PT_b = E_OUT[eset]["PTt"]
        KTt, KT_b = E_OUT[eset]["KTt"]
        VTt, VT_b = E_OUT[eset]["VTt"]
        kk, kk_b = W_["kk"]
        kkn, kkn_b = W_["kkn"]
        ka, ka_b = W_["ka"]
        k2, k2_b = W_["k2"]
        pp, pp_b = W_["pp"]
        bv, bv_b = E_OUT[eset]["bv"]
        Gs, Gs_b = W_["Gs"]
        eG, eG_b = E_OUT[eset]["eG"]
        enG, enG_b = W_["enG"]
        eGx, eGx_b = W_["eGx"]
        yn, yn_b = W_["yn"]
        if "E" in part:
            _rw_E(locals())
        if "C" in part:
            _rw_C(locals())

    def _rw_E(L_):
        globals_ = L_
        (T, C, nch, L, so, full_proj, first, par, par_b, wcol, wcol_b, wda_ap, wda_b, wg_ap, wg_b, init_prev, H_init,
         zcar, zcar_b) = [L_[k] for k in ("T", "C", "nch", "L", "so", "full_proj", "first", "par", "par_b", "wcol", "wcol_b",
                                          "wda_ap", "wda_b", "wg_ap", "wg_b", "init_prev", "H_init", "zcar", "zcar_b")]
        mu5 = L_["mu5"]
        (tmp, tmp_b, tmp2, tmp2_b, sgw, sgw_b, aa, aa_b, gate, gate_b, QR, QR_b, PTt, PT_b, KTt, KT_b, VTt, VT_b, kk, kk_b,
         kkn, kkn_b, ka, ka_b, k2, k2_b, pp, pp_b, bv, bv_b, Gs, Gs_b, eG, eG_b, enG, enG_b, eGx, eGx_b) = [L_[k] for k in (
            "tmp", "tmp_b", "tmp2", "tmp2_b", "sgw", "sgw_b", "aa", "aa_b", "gate", "gate_b", "QR", "QR_b", "PTt", "PT_b",
            "KTt", "KT_b", "VTt", "VT_b", "kk", "kk_b", "kkn", "kkn_b", "ka", "ka_b", "k2", "k2_b", "pp", "pp_b", "bv", "bv_b",
            "Gs", "Gs_b", "eG", "eG_b", "enG", "enG_b", "eGx", "eGx_b")]
        if first:
            init_prev()
            H_init()
        else:
            r0_, r1_ = (1, 4) if so else (0, 5)
            cp(zr[:, r0_:r1_, 0:1], zcar[:, r0_:r1_, 0:1], [zcar_b], [zr_b])
        CL = (1, 2, 3) if so else (0, 1, 2, 3, 4)
        CLp = (0, 1, 2, 3, 4) if (full_proj or not so) else CL
        for c in CLp:
            pz, pz_b = next_z()
            for kc in range(8):
                mm(pz[:, 0:T], wcol(c, kc), hT[:, kc, 0:T], wcol_b(kc) + [hT_b], [pz_b], start=(kc == 0), stop=(kc == 7))
            cp(zr[:, c, 1:T + 1], pz[:, 0:T], [pz_b], [zr_b], eng="act")
        if zcar is not None:
            w0_, w1_ = (0, 5) if (full_proj or not so) else (1, 4)
            cp(zcar[:, w0_:w1_, 0:1], zr[:, w0_:w1_, T:T + 1], [zr_b], [zcar_b])
        rchk(1)
        c0, c1 = (1, 4) if so else (0, 5)
        mub = mu5[:, c0:c1].unsqueeze(2).to_broadcast([128, c1 - c0, T])
        tt(zs[:, c0:c1, 0:T], zr[:, c0:c1, 0:T], zr[:, c0:c1, 1:T + 1], ALU.subtract, [zr_b], [zs_b])
        tt(zs[:, c0:c1, 0:T], zs[:, c0:c1, 0:T], mub, ALU.mult, [zs_b, mu5_b], [zs_b])
        tt(zs[:, c0:c1, 0:T], zs[:, c0:c1, 0:T], zr[:, c0:c1, 1:T + 1], ALU.add, [zs_b, zr_b], [zs_b])
        rchk(2)
        def sig_act(dst, src, R, W, pr, scale=-1.0, bias=None, wk=None, wk_b=None):
            act(wk[pr, 0:T], src, AF.Exp, R, [wk_b], scale=scale, bias=bias)
            act(wk[pr, 0:T], wk[pr, 0:T], AF.Ln, [wk_b, cm_b], [wk_b], bias=ONES[pr, 0:1])
            act(dst, wk[pr, 0:T], AF.Exp, [wk_b], W, scale=-1.0)
        p0 = slice(0, 64)
        pa_ = slice(0, 128)
        sig_act(tmp[p0, 0:T], zs[p0, 3, 0:T], [zs_b], [tmp_b], p0, scale=-2.0, wk=tmp, wk_b=tmp_b)
        ts(lwa[0:64, 0:T], tmp[0:64, 0:T], 2.0, -1.0, ALU.mult, ALU.add, [tmp_b], [lwa_b])
        cp(lwa[64:128, 0:T], zs[64:128, 3, 0:T], [zs_b], [lwa_b], eng="act")
        if not so:
            sig_act(sg[:, 0:T], zs[:, 4, 0:T], [zs_b], [sg_b], pa_, wk=tmp2, wk_b=tmp2_b)
        rchk(3)
        pz, pz_b = next_z()
        mm(pz[:, 0:T], wda_ap[0:64, :], lwa[0:64, 0:T], [wda_b, lwa_b], [pz_b])
        sig_act(sgw[:, 0:T], pz[:, 0:T], [pz_b, par_b], [sgw_b], pa_, bias=par[:, 10:11], wk=tmp, wk_b=tmp_b)
        pz, pz_b = next_z()
        mm(pz[:, 0:T], wda_ap[64:128, :], lwa[64:128, 0:T], [wda_b, lwa_b], [pz_b])
        sig_act(aa[:, 0:T], pz[:, 0:T], [pz_b, par_b], [aa_b], pa_, bias=par[:, 11:12], wk=tmp2, wk_b=tmp2_b)
        if not so:
            pz, pz_b = next_z()
            mm(pz[:, 0:T], wg_ap, sg[:, 0:T], [wg_b, sg_b], [pz_b])
            cp(gate[:, 0:T], pz[:, 0:T], [pz_b], [gate_b], eng="act")
        rchk(4)
        act(tmp[:, 0:T], zs[:, 1, 0:T], AF.Square, [zs_b, par_b], [tmp_b], scale=par[:, 5:6])
        pz, pz_b = next_z()
        mm(pz[:, 0:T], BONES, tmp[:, 0:T], [cm_b, tmp_b], [pz_b])
        ts(tmp2[:, 0:T], pz[:, 0:T], 1e-18, None, ALU.max, None, [pz_b], [tmp2_b])
        act(tmp2[:, 0:T], tmp2[:, 0:T], AF.Ln, [tmp2_b], [tmp2_b])
        act(tmp2[:, 0:T], tmp2[:, 0:T], AF.Exp, [tmp2_b], [tmp2_b], scale=-0.5)
        stt(kkn[:, 0:T], zs[:, 1, 0:T], par[:, 5:6], tmp2[:, 0:T], ALU.mult, ALU.mult, [zs_b, par_b, tmp2_b], [kkn_b])
        rchk(5)
        ts(ka[:, 0:T], aa[:, 0:T], -1.0, None, ALU.add, None, [aa_b], [ka_b])
        ts(ka[:, 0:T], ka[:, 0:T], par[:, 6:7], ONES[:, 0:1], ALU.mult, ALU.add, [ka_b, par_b, cm_b], [ka_b])
        tt(k2[:, 0:T], zs[:, 1, 0:T], ka[:, 0:T], ALU.mult, [zs_b, ka_b], [k2_b])
        tt(pp[:, 0:T], aa[:, 0:T], kkn[:, 0:T], ALU.mult, [aa_b, kkn_b], [pp_b])
        rchk(6)
        if not so:
            stt(tmp[:, 0:T], zs[:, 0, 0:T], par[:, 7:8], k2[:, 0:T], ALU.mult, ALU.mult, [zs_b, par_b, k2_b], [tmp_b])
            pz, pz_b = next_z()
            mm(pz[:, 0:T], BONES, tmp[:, 0:T], [cm_b, tmp_b], [pz_b])
            tt(bv[:, 0:T], pz[:, 0:T], zs[:, 2, 0:T], ALU.mult, [pz_b, zs_b], [bv_b])
        rchk(7)
        for ci in range(nch):
            cs = slice(ci * C, (ci + 1) * C)
            P.add("dve", lambda e, cs=cs: e.tensor_tensor_scan(out=Gs[:, cs], data0=ONES[:, 0:C], data1=sgw[:, cs],
                                                               initial=0.0, op0=ALU.mult, op1=ALU.add),
                  [cm_b, sgw_b], [Gs_b])
        tt(tmp[:, 0:T], Gs[:, 0:T], sgw[:, 0:T], ALU.subtract, [Gs_b, sgw_b], [tmp_b])
        act(eG[:, 0:T], Gs[:, 0:T], AF.Exp, [Gs_b], [eG_b], scale=-DSC)
        act(enG[:, 0:T], Gs[:, 0:T], AF.Exp, [Gs_b], [enG_b], scale=DSC)
        act(eGx[:, 0:T], tmp[:, 0:T], AF.Exp, [tmp_b], [eGx_b], scale=-DSC)
        rchk(8)
        v3 = lambda ap: ap.rearrange("p (n c) -> p n c", c=C)
        tt(QR[:, 0:nch, 0:C], v3(kkn[:, 0:T]), v3(eGx[:, 0:T]), ALU.mult, [kkn_b, eGx_b], [QR_b])
        if not so:
            tt(QR[:, 0:nch, C:2 * C], v3(zs[:, 0, 0:T]), v3(eG[:, 0:T]), ALU.mult, [zs_b, eG_b], [QR_b])
        tt(PTt[:, 0:T], pp[:, 0:T], enG[:, 0:T], ALU.mult, [pp_b, enG_b], [PT_b])
        tt(KTt[:, 0:T], k2[:, 0:T], enG[:, 0:T], ALU.mult, [k2_b, enG_b], [KT_b])
        cp(VTt[:, 0:T], zs[:, 2, 0:T], [zs_b], [VT_b], eng="act")

        rchk(9)

    def _rw_C(L_):
        (T, C, nch, L, so, last_out, par, par_b, y_dst, y_dst_b, Hbf, Hbf_b, Hf, Hf_b) = [L_[k] for k in (
            "T", "C", "nch", "L", "so", "last_out", "par", "par_b", "y_dst", "y_dst_b", "Hbf", "Hbf_b", "Hf", "Hf_b")]
        (gate, gate_b, QR, QR_b, PTt, PT_b, KTt, KT_b, VTt, VT_b, bv, bv_b, eG, eG_b, yn, yn_b) = [L_[k] for k in (
            "gate", "gate_b", "QR", "QR_b", "PTt", "PT_b", "KTt", "KT_b", "VTt", "VT_b", "bv", "bv_b", "eG", "eG_b", "yn", "yn_b")]
        chains = [(ci, h) for ci in range(nch) for h in range(2)]

        def bank(ci, h):
            return ps_c[2 * ci + h], ps_c_b[2 * ci + h]
        CS = [slice(ci * C, (ci + 1) * C) for ci in range(nch)]
        NQ = C if so else 2 * C
        for ci in range(nch):
            cs = CS[ci]
            TOK, TOK_b = TOKs[ci]
            srcs = [(QR[:, ci, 0:C], QR_b), (PTt[:, cs], PT_b), (KTt[:, cs], KT_b), (VTt[:, cs], VT_b)]
            for q_, (s_ap, s_b) in enumerate(srcs):
                tr(ps_t2[:C, q_ * 128:(q_ + 1) * 128], s_ap, cmb[:, :], [s_b, cmb_b], [ps_t2_b])
            cp(TOK[:C, 1:4, :], ps_t2[:C, 128:512].rearrange("p (q c) -> p q c", q=3), [ps_t2_b], [TOK_b])
            for h in range(2):
                QY0, QY0_b = chn[ci][h]["QYa"]
                qc = slice(0, 64) if h == 0 else slice(64, 128)
                cp(QY0[:C, qc], ps_t2[:C, 64 * h:64 * h + 64], [ps_t2_b], [QY0_b], eng="act")
        rchk(10)
        for ci, h in chains:
            c_ = chn[ci][h]
            cs = CS[ci]
            sl = slice(64 * h, 64 * h + 64)
            bk, bb = bank(ci, h)
            mm(bk[:C, 0:NQ], PTt[sl, cs], QR[sl, ci, 0:NQ], [PT_b, QR_b], [bb[0], bb[1]])
            mm(bk[:C, 256:256 + NQ], KTt[sl, cs], QR[sl, ci, 0:NQ], [KT_b, QR_b], [bb[2], bb[3]])
            tt(c_["AqpT"][0][:C, :C], bk[:C, 0:C], MUS[:C, :C], ALU.mult, [bb[0], bb[1], cm_b], [c_["AqpT"][1]])
            if not so:
                tt(c_["ArpT"][0][:C, :C], bk[:C, C:2 * C], MUI[:C, :C], ALU.mult, [bb[0], bb[1], cm_b], [c_["ArpT"][1]])
            tt(c_["AqkTn"][0][:C, :C], bk[:C, 256:256 + C], MUSN[:C, :C], ALU.mult, [bb[2], bb[3], cm_b], [c_["AqkTn"][1]])
            if not so:
                tt(c_["ArkT"][0][:C, :C], bk[:C, 256 + C:256 + 2 * C], MUI[:C, :C], ALU.mult, [bb[2], bb[3], cm_b], [c_["ArkT"][1]])
        for ci, h in chains:
            c_ = chn[ci][h]
            cs = CS[ci]
            sl = slice(64 * h, 64 * h + 64)
            bk, bb = bank(ci, h)
            mm(bk[:C, 0:C], QR[sl, ci, 0:C], PTt[sl, cs], [QR_b, PT_b], [bb[0]])
            tt(c_["A"][0][:C, :C], bk[:C, 0:C], MLS[:C, :C], ALU.mult, [bb[0], cm_b], [c_["A"][1]])
        rchk(11)
        for ci, h in chains:
            c_ = chn[ci][h]
            sl = slice(64 * h, 64 * h + 64)
            bk, bb = bank(ci, h)
            TOK, TOK_b = TOKs[ci]
            yc_ = slice(64, 128) if h == 0 else slice(0, 64)
            QY0, QY0_b = c_["QYa"]
            mm(bk[:C, 384:448], c_["AqkTn"][0][:C, :C], TOK[:C, 3, sl], [c_["AqkTn"][1], TOK_b], [bb[3]])
            cp(QY0[:C, yc_], bk[:C, 384:448], [bb[3]], [QY0_b], eng="act")
        rchk(12)
        cur = {}
        for ci, h in chains:
            c_ = chn[ci][h]
            bk, bb = bank(ci, h)
            mm(bk[:C, 256:384], c_["AqpT"][0][:C, :C], c_["QYa"][0][:C, :], [c_["AqpT"][1], c_["QYa"][1]], [bb[2]])
            tt(c_["QYb"][0][:C, :], c_["QYa"][0][:C, :], bk[:C, 256:384], ALU.subtract, [c_["QYa"][1], bb[2]], [c_["QYb"][1]])
            cur[(ci, h)] = "QYb"
        rchk(13)
        pcur = ("A", "AqpT")
        for j in range(1, L + 1):
            pn = ("Pa", "PaT") if j % 2 == 1 else ("Pb", "PbT")
            Pp, PpT = pcur
            for k_, (ci, h) in enumerate(chains):
                c_ = chn[ci][h]
                bk, bb = bank(ci, h)
                if j < L:
                    mm(bk[:C, 0:C], c_[PpT][0][:C, :C], c_[Pp][0][:C, :C], [c_[PpT][1], c_[Pp][1]], [bb[0]])
                mm(bk[:C, 128:128 + C], c_[Pp][0][:C, :C], c_[PpT][0][:C, :C], [c_[PpT][1], c_[Pp][1]], [bb[1]])
            for k_, (ci, h) in enumerate(chains):
                c_ = chn[ci][h]
                bk, bb = bank(ci, h)
                e1, e2 = ("act", "dve") if k_ % 2 == 0 else ("dve", "act")
                if j < L:
                    cp(c_["PP" + pn[0][1]][0][:C, 0:256], bk[:C, 0:256], [bb[0], bb[1]], [c_[pn[0]][1]], eng=e1)
                else:
                    cp(c_[pn[1]][0][:C, :C], bk[:C, 128:128 + C], [bb[1]], [c_[pn[1]][1]], eng=e2)
            for ci, h in chains:
                c_ = chn[ci][h]
                bk, bb = bank(ci, h)
                src = cur[(ci, h)]
                dst = "QYa" if src == "QYb" else "QYb"
                mm(bk[:C, 256:384], c_[pn[1]][0][:C, :C], c_[src][0][:C, :], [c_[pn[1]][1], c_[src][1]], [bb[2]])
                tt(c_[dst][0][:C, :], c_[src][0][:C, :], bk[:C, 256:384], ALU.add, [c_[src][1], bb[2]], [c_[dst][1]])
                cur[(ci, h)] = dst
            pcur = pn
        rchk(14)
        for ci in range(nch):
            cs = CS[ci]
            TOK, TOK_b = TOKs[ci]
            ON, ON_b = ONs[ci]
            MTp, MTp_b = MTps[ci]
            for h in range(2):
                c_ = chn[ci][h]
                sl = slice(64 * h, 64 * h + 64)
                bk, bb = bank(ci, h)
                WU, WU_b = c_[cur[(ci, h)]]
                if h == 0:
                    Wl, Ul, orow = WU[:C, 0:64], WU[:C, 64:128], slice(0, 64)
                else:
                    Wl, Ul, orow = WU[:C, 0:128], WU[:C, 0:64], slice(0, 128)
                if not so:
                    mm(bk[orow, 0:C], Wl, c_["ArpT"][0][:C, :C], [WU_b, c_["ArpT"][1]], [bb[0]])
                    tt(RfT[sl, cs], QR[sl, ci, C:2 * C], bk[sl, 0:C], ALU.subtract, [QR_b, bb[0]], [RfT_cb[ci]])
                mm(bk[orow, 128:192], Wl, TOK[:C, 1, sl], [WU_b, TOK_b], [bb[1]])
                tt(MTp[sl, sl], IDF[sl, sl], bk[sl, 128:192], ALU.subtract, [cm_b, bb[1]], [MTp_b])
                if not so:
                    mm(bk[:C, 384:448], c_["ArpT"][0][:C, :C], Ul, [c_["ArpT"][1], WU_b], [bb[3]], start=True, stop=False)
                    mm(bk[:C, 384:448], c_["ArkT"][0][:C, :C], TOK[:C, 3, sl], [c_["ArkT"][1], TOK_b], [bb[3]], start=False, stop=False)
                    mm(bk[:C, 384:448], RfT[sl, cs], Hbf[sl, :], [RfT_cb[ci], Hbf_b], [bb[3]], start=False, stop=True)
                    st_, st_b = c_["st"]
                    mv_, mv_b = c_["mv"]
                    P.add("dve", lambda e, bk=bk, st_=st_: e.bn_stats(out=st_[:C, 0:6], in_=bk[:C, 384:448]), [bb[3]], [st_b])
                    P.add("dve", lambda e, st_=st_, mv_=mv_: e.bn_aggr(out=mv_[:C, 0:2], in_=st_[:C, 0:6]), [st_b], [mv_b])
                    ts(mv_[:C, 2:3], mv_[:C, 1:2], GEPS, None, ALU.add, None, [mv_b], [mv_b])
                    act(mv_[:C, 2:3], mv_[:C, 2:3], AF.Ln, [mv_b], [mv_b])
                    act(mv_[:C, 3:4], mv_[:C, 2:3], AF.Exp, [mv_b], [mv_b], scale=-0.5)
                    ts(ON[:C, sl], bk[:C, 384:448], mv_[:C, 0:1], mv_[:C, 3:4], ALU.subtract, ALU.mult, [bb[3], mv_b], [ON_b])
                mm(bk[:, 256:320], TOK[:C, 1, :], Ul, [TOK_b, WU_b], [bb[2]], start=True, stop=False)
                mm(bk[:, 256:320], TOK[:C, 2, :], TOK[:C, 3, sl], [TOK_b], [bb[2]], start=False, stop=False)
                mm(bk[:, 256:320], MTp[sl, :], Hbf[sl, :], [MTp_b, Hbf_b], [bb[2]], start=False, stop=True)
                gl = eG[sl, ci * C + C - 1:ci * C + C]
                ts(Hbf[sl, :], bk[sl, 256:320], gl, None, ALU.mult, None, [bb[2], eG_b], [Hbf_b])
                if last_out and ci == nch - 1:
                    ts(Hf[sl, :], bk[sl, 256:320], gl, None, ALU.mult, None, [bb[2], eG_b], [Hf_b])
        rchk(15)
        if not so:
            for ci in range(nch):
                cs = CS[ci]
                ON, ON_b = ONs[ci]
                tr(ps_tf[:, 0:C], ON[:C, :], IDF[:C, :C], [ON_b, cm_b], [ps_tf_b])
                ts(yn[:, cs], ps_tf[:, 0:C], par[:, 8:9], par[:, 9:10], ALU.mult, ALU.add, [ps_tf_b, par_b], [yn_b])
            tt(yn[:, 0:T], yn[:, 0:T], bv[:, 0:T], ALU.add, [yn_b, bv_b], [yn_b])
            tt(y_dst, yn[:, 0:T], gate[:, 0:T], ALU.mult, [yn_b, gate_b], [y_dst_b])

    def conv_group(T, wb_list, first_init):
        if first_init is not None:
            first_init()
        else:
            cp(ucv[:, :, 0:2], ucv[:, :, 512:514], [ucv_b], [ucv_b])
        for cq in range(4):
            pzh, pzh_b = next_z()
            for kc in range(8):
                mm(pzh[:, 0:T], wcv[:, kc, (8 + cq) * 128:(9 + cq) * 128], hT[:, kc, 0:T], [wcv_kb[kc], wcv_kb2[kc], hT_b], [pzh_b],
                   start=(kc == 0), stop=(kc == 7))
            cp(zhs[:, 0:T], pzh[:, 0:T], [pzh_b], [zhs_b], eng="act")
            pzc, pzc_b = next_z()
            for kc in range(8):
                mm(pzc[:, 0:T], wcv[:, kc, (4 + cq) * 128:(5 + cq) * 128], hT[:, kc, 0:T], [wcv_kb[kc], wcv_kb2[kc], hT_b], [pzc_b],
                   start=(kc == 0), stop=(kc == 7))
            tt(ucv[:, cq, 2:2 + T], pzc[:, 0:T], zhs[:, 0:T], ALU.mult, [pzc_b, zhs_b], [ucv_b])
            ts(yct[:, 0:T], ucv[:, cq, 0:T], cvw[:, cq, 0:1], None, ALU.mult, None, [ucv_b, cvw_b], [yct_b])
            stt(yct[:, 0:T], ucv[:, cq, 1:1 + T], cvw[:, cq, 1:2], yct[:, 0:T], ALU.mult, ALU.add, [ucv_b, cvw_b, yct_b], [yct_b])
            stt(yct[:, 0:T], ucv[:, cq, 2:2 + T], cvw[:, cq, 2:3], yct[:, 0:T], ALU.mult, ALU.add, [ucv_b, cvw_b, yct_b], [yct_b])
            pzb, pzb_b = next_z()
            for kc in range(8):
                mm(pzb[:, 0:T], wcv[:, kc, cq * 128:(cq + 1) * 128], hT[:, kc, 0:T], [wcv_kb[kc], wcv_kb2[kc], hT_b], [pzb_b],
                   start=(kc == 0), stop=(kc == 7))
            tt(ycv[:, cq, 0:T], pzb[:, 0:T], yct[:, 0:T], ALU.mult, [pzb_b, yct_b], [ycv_b])

    wrot = [0]

    def wout_tile(tn, mix_fn, x_ap, x1_dst_row0):
        wrot[0] ^= 1
        bM = [ps_c[2 * wrot[0]], ps_c[2 * wrot[0] + 1]]
        bMb = [ps_c_b[2 * wrot[0]], ps_c_b[2 * wrot[0] + 1]]
        for half in range(2):
            for c in range(8):
                ap_, b_ = mix_fn(c)
                mm(bM[half][:tn, :], ap_, wo[:, c, half * 512:(half + 1) * 512], [b_, wo_kb[c]], bMb[half],
                   start=(c == 0), stop=(c == 7))
        x_t, x_b = xt[0]
        dma(x_t[:tn, :], x_ap, [], [x_b], x_b)
        for half in range(2):
            act(junk[:tn, 0:512], bM[half][:tn, :], AF.Square, bMb[half], [junk_b, sm2_b], accum=sm2[:tn, half:half + 1])
        tt(sm2[:tn, 2:3], sm2[:tn, 0:1], sm2[:tn, 1:2], ALU.add, [sm2_b], [sm2_b])
        r = rstd_from(sm2[:tn, 2:3], tn, 1.0 / D, NEPS, sm2_b)
        for half in range(2):
            hs = slice(half * 512, (half + 1) * 512)
            stt(t1k[:tn, hs], bM[half][:tn, :], r, gpost[:tn, hs], ALU.mult, ALU.mult, bMb[half] + [sm_b, gpost_b], [t1k_b])
        tt(x1t[:tn, :], t1k[:tn, :], x_t[:tn, :], ALU.add, [t1k_b, x_b], [x1t_b])
        dma(x1scr[x1_dst_row0:x1_dst_row0 + tn, :], x1t[:tn, :], [x1t_b], [x1scr_b], x1t_b)

    def _phases(chk, stacks):
        import os
        def pair_args(q):
            zcols = [q * 128, 512 + q * 128, 1024 + q * 128, 1536, 1664]

            def wcol(c, kc, zcols=zcols):
                return wrw[:, kc, zcols[c]:zcols[c] + 128]
            return dict(wcol=wcol, wcol_b=lambda kc: [wrw_kb[kc], wrw_kb2[kc]], par=parf[:, q, :], par_b=parf_b,
                        wda_ap=wdaf[:, q * 128:(q + 1) * 128], wda_b=wdaf_b, wg_ap=wgf[:, q * 128:(q + 1) * 128], wg_b=wgf_b,
                        Hbf=Hbf4[:, q, :], Hbf_b=Hbf4_b[q], Hf=Hf4[:, q, :], Hf_b=Hf4_b[q], mu5=mu5all[:, q, :])

        chk(1)
        NPRE = int(os.environ.get("KDBG_NPRE", str(3 * SEG // TG)))
        NOWN = int(os.environ.get("KDBG_NOWN", str(SEG // TG)))
        calls = [(g, q) for g in range(NPRE + NOWN) for q in range(4)]

        def call_kw(g, q):
            own = g >= NPRE
            zc_t, zc_b = zcar4[q]

            def init_prev():
                P.add("dve", lambda e: e.memset(zr[:, :, 0:1], 0.0), [], [zr_b])

            def H_init(q=q):
                P.add("dve", lambda e, q=q: e.memset(Hbf4[:, q, :], 0.0), [], [Hbf4_b[q]])
            ydst = ygt[:, q, (g - NPRE) * TG:(g - NPRE + 1) * TG] if own else None
            kw = dict(T=TG, init_prev=init_prev, H_init=H_init, y_dst=ydst, y_dst_b=ygt_b, first=(g == 0),
                      last_out=(g == NPRE + NOWN - 1), zcar=zc_t, zcar_b=zc_b, so=(not own), full_proj=(g == NPRE - 1))
            kw.update(pair_args(q))
            return kw

        def emit_norm(g):
            if g >= NPRE:
                r0 = (g - NPRE) * TG
                norm_transpose(lambda ti, r0=r0: xseg[r0 + ti * 128:r0 + (ti + 1) * 128, :], TG // 128, 128, hT, hT_b)
            else:
                norm_transpose(lambda ti, g=g: xpre[g * TG + ti * 128:g * TG + (ti + 1) * 128, :], TG // 128, 128, hT, hT_b)

        def emitE(n):
            g, q = calls[n]
            if q == 0:
                emit_norm(g)
            rwkv_pair(part="E", eset=n % 2, **call_kw(g, q))

        def emitC(n):
            g, q = calls[n]
            rwkv_pair(part="C", eset=n % 2, **call_kw(g, q))
        PIPE = os.environ.get("KDBG_NOPIPE") != "1"
        if len(calls) > 0:
            emitE(0)
        for n in range(len(calls)):
            if PIPE:
                P.cap = []
                emitC(n)
                A_ = P.cap
                P.cap = []
                if n + 1 < len(calls):
                    emitE(n + 1)
                B_ = P.cap
                P.cap = None
                P.merge(A_, B_)
            else:
                emitC(n)
                if n + 1 < len(calls):
                    emitE(n + 1)
        chk(2)
        for q in range(4):
            out_dma_ops.append(dma(H_p[:, q, :], Hf4[:, q, :], [Hf4_b[q]], [], Hf4_b[q]))
            out_dma_ops.append(dma(zl_p[:, q, :], zcar4[q][0][:, :, 0:1].rearrange("p c o -> p (c o)"), [zcar4[q][1]], [], zcar4[q][1]))
        chk(3)

        chk(0)
        norm_transpose(lambda ti: xs_in[:, :], 1, TS, hT, hT_b)
        chk(10)

        def s_conv_init():
            dma(ucv[:, :, 0:2], sconv, [], [ucv_b], ucv_b)
        conv_group(TS, None, s_conv_init)
        out_dma_ops.append(dma(ucv_s, ucv[:, :, TS:TS + 2], [ucv_b], [], ucv_b))
        chk(11)
        for q in range(4):
            def init_prev(q=q):
                for c, off in enumerate([q, 4 + q, 8 + q, 12, 13]):
                    dma(zr[:, c, 0:1], sshift[:, off:off + 1], [], [zr_b], zr_b)

            def H_init(q=q):
                dma(Hf4[:, q, :], swkv[:, q, :], [], [Hf4_b[q]], Hf4_b[q])
                cp(Hbf4[:, q, :], Hf4[:, q, :], [Hf4_b[q]], [Hbf4_b[q]])
            rwkv_pair(TS, init_prev=init_prev, H_init=H_init, y_dst=ysm[:, q, :], y_dst_b=ysm_b, first=True, last_out=True,
                      **pair_args(q))
            out_dma_ops.append(dma(H_s[:, q, :], Hf4[:, q, :], [Hf4_b[q]], [], Hf4_b[q]))
            out_dma_ops.append(dma(zl_s[:, q, :], zr[:, :, TS:TS + 1].rearrange("p c o -> p (c o)"), [zr_b], [], zr_b))
            chk(12 + q)

        def s_mix(c):
            if c < 4:
                return ycv[:, c, 0:TS], ycv_b
            return ysm[:, c - 4, :], ysm_b
        wout_tile(TS, s_mix, xs_in[:, :], SEG)


        norm_transpose(lambda ti: xhalo[:, :], 1, 2, hT, hT_b)

        def p_conv_init():
            for cq in range(4):
                pzh, pzh_b = next_z()
                for kc in range(8):
                    mm(pzh[:, 0:2], wcv[:, kc, (8 + cq) * 128:(9 + cq) * 128], hT[:, kc, 0:2], [wcv_kb[kc], wcv_kb2[kc], hT_b], [pzh_b],
                       start=(kc == 0), stop=(kc == 7))
                cp(zhs[:, 0:2], pzh[:, 0:2], [pzh_b], [zhs_b], eng="act")
                pzc, pzc_b = next_z()
                for kc in range(8):
                    mm(pzc[:, 0:2], wcv[:, kc, (4 + cq) * 128:(5 + cq) * 128], hT[:, kc, 0:2], [wcv_kb[kc], wcv_kb2[kc], hT_b], [pzc_b],
                       start=(kc == 0), stop=(kc == 7))
                tt(ucv[:, cq, 0:2], pzc[:, 0:2], zhs[:, 0:2], ALU.mult, [pzc_b, zhs_b], [ucv_b])
        p_conv_init()
        for g in range(SEG // 512):
            norm_transpose(lambda ti, g=g: xseg[g * 512 + ti * 128:g * 512 + (ti + 1) * 128, :], 4, 128, hT, hT_b)
            conv_group(512, None, (lambda: None) if g == 0 else None)
            for ti in range(4):
                t0 = g * 512 + ti * 128

                def p_mix(c, ti=ti, t0=t0):
                    if c < 4:
                        return ycv[:, c, ti * 128:(ti + 1) * 128], ycv_b
                    return ygt[:, c - 4, t0:t0 + 128], ygt_b
                wout_tile(128, p_mix, xseg[t0:t0 + 128, :], t0)
        out_dma_ops.append(dma(ucv_p, ucv[:, :, 512:514], [ucv_b], [], ucv_b))

        chk(4)
        P.barrier()
        stacks.pop("S").close()
        stacks.pop("A").close()
        esB = ExitStack()
        stacks["B"] = esB
        w1, w1_b = sb(esB, "w1", [128, 8, DFF], BF16)
        w2, w2_b = sb(esB, "w2", [128, 32, D], BF16)
        stgB = [sb(esB, f"stgB{i}", [128, 1024]) for i in range(2)]
        aT, aT_b = sb(esB, "aT", [128, 32, 256], BF16)
        x1g = [sb(esB, f"x1g{i}", [128, D]) for i in range(2)]
        yo = [sb(esB, f"yo{i}", [128, D]) for i in range(1)]
        stgB4 = stgB + [yo[0], x1g[1]]
        srot[0] = 0
        w1_kb = [[Buf(f"w1_{k}_{h}") for h in range(4)] for k in range(8)]
        w2_fb = [Buf(f"w2_{f}") for f in range(32)]
        for hf in range(4):
            for kc in range(8):
                load_cast(w1[:, kc, hf * 1024:(hf + 1) * 1024], w1_kb[kc][hf], w_ff1[kc * 128:(kc + 1) * 128, hf * 1024:(hf + 1) * 1024],
                          1024, gpre_t[:, 8 + kc:9 + kc], gpre_b, stg_=stgB4)
        for fc in range(32):
            load_cast(w2[:, fc, :], w2_fb[fc], w_ff2[fc * 128:(fc + 1) * 128, :], 1024, stg_=stgB4)

        def ffn_group(row0, ntiles, tn, ydst):
            T = (ntiles - 1) * 128 + tn
            for ti in range(ntiles):
                x_t, x_b = x1g[ti]
                dma(x_t[:tn, :], x1scr[row0 + ti * 128:row0 + ti * 128 + tn, :], [x1scr_b], [x_b], x_b)
                act(junk[:tn, :], x_t[:tn, :], AF.Square, [x_b], [junk_b, sm_b], accum=sm[:tn, 0:1])
                r = rstd_from(sm[:tn, 0:1], tn, 1.0 / D, NEPS, sm_b)
                act(xn[:tn, :], x_t[:tn, :], AF.Copy, [x_b, sm_b], [xn_b], scale=r)
                for kc in range(8):
                    tr(ps_tr[:, kc * 128:kc * 128 + tn], xn[:tn, kc * 128:(kc + 1) * 128], cmb[:tn, :tn], [xn_b, cmb_b], [ps_tr_b])
                src = ps_tr[:, :].rearrange("p (k t) -> p k t", k=8)[:, :, 0:tn]
                cp(hT[:, :, ti * 128:ti * 128 + tn], src, [ps_tr_b], [hT_b], eng=("act" if ti % 2 else "dve"))
            for fc in range(32):
                pz, pz_b = next_z()
                for kc in range(8):
                    mm(pz[:, 0:T], w1[:, kc, fc * 128:(fc + 1) * 128], hT[:, kc, 0:T], [w1_kb[kc][fc // 8], hT_b], [pz_b],
                       start=(kc == 0), stop=(kc == 7))
                if fc % 2 == 0:
                    ts(t1k[:, 0:T], pz[:, 0:T], 0.0, None, ALU.max, None, [pz_b], [t1k_b])
                    tt(aT[:, fc, 0:T], t1k[:, 0:T], t1k[:, 0:T], ALU.mult, [t1k_b], [aT_b])
                else:
                    act(x1t[:, 0:T], pz[:, 0:T], AF.Relu, [pz_b], [x1t_b])
                    tt(aT[:, fc, 0:T], x1t[:, 0:T], x1t[:, 0:T], ALU.mult, [x1t_b], [aT_b], eng="pool")
            for ti in range(ntiles):
                x_t, x_b = x1g[ti]
                bM = [ps_c[2 * (ti % 2)], ps_c[2 * (ti % 2) + 1]]
                bMb = [ps_c_b[2 * (ti % 2)], ps_c_b[2 * (ti % 2) + 1]]
                for half in range(2):
                    for fc in range(32):
                        mm(bM[half][:tn, :], aT[:, fc, ti * 128:ti * 128 + tn], w2[:, fc, half * 512:(half + 1) * 512],
                           [aT_b, w2_fb[fc]], bMb[half], start=(fc == 0), stop=(fc == 31))
                for half in range(2):
                    act(junk[:tn, 0:512], bM[half][:tn, :], AF.Square, bMb[half], [junk_b, sm2_b], accum=sm2[:tn, half:half + 1])
                tt(sm2[:tn, 2:3], sm2[:tn, 0:1], sm2[:tn, 1:2], ALU.add, [sm2_b], [sm2_b])
                r = rstd_from(sm2[:tn, 2:3], tn, 1.0 / D, NEPS, sm2_b)
                y_t, y_b = yo[0]
                for half in range(2):
                    hs = slice(half * 512, (half + 1) * 512)
                    stt(y_t[:tn, hs], bM[half][:tn, :], r, gffn[:tn, hs], ALU.mult, ALU.mult, bMb[half] + [sm_b, gffn_b], [y_b])
                tt(y_t[:tn, :], y_t[:tn, :], x_t[:tn, :], ALU.add, [y_b, x_b], [y_b])
                out_dma_ops.append(dma(ydst[ti * 128:ti * 128 + tn, :] if ydst is y_s else ydst[row0 + ti * 128:row0 + ti * 128 + tn, :],
                                       y_t[:tn, :], [y_b], [], y_b))

        for g in range(SEG // 256):
            ffn_group(g * 256, 2, 128, y_seg)
        ffn_group(SEG, 1, TS, y_s)


    import os
    KSTOP = int(os.environ.get("KDBG_STOP", "99"))

    def chk(n):
        if KSTOP == n:
            raise Stop()
    stacks = {"A": esA, "S": esS}
    try:
        _phases(chk, stacks)
    except Stop:
        pass
    P.finalize(out_dma_ops)
    with nc.allow_low_precision(reason="bf16 matmul operands by design"), \
            nc.allow_non_contiguous_dma(reason="tiny state vectors"), nc.Block() as block:
        P.emit(block)
    for k in ("B", "P", "S", "A"):
        if k in stacks:
            stacks[k].close()
    es.close()
    return nc


_NC = None


def _masks():
    C = 128
    i = np.arange(C)
    m = np.zeros((128, 7, 128), np.float32)
    m[:, 0, :] = np.eye(C)
    m[:, 1, :] = (i[:, None] < i[None, :])
    m[:, 2, :] = (i[:, None] <= i[None, :])
    m[:, 3, :] = -(i[:, None] < i[None, :]).astype(np.float32)
    m[:, 4, :] = (i[:, None] > i[None, :])
    bo = np.zeros((128, 128), np.float32)
    bo[:64, :64] = 1
    bo[64:, 64:] = 1
    m[:, 5, :] = bo
    m[:, 6, :] = 1.0
    return m


def kernel(x_prompt, x_sample, state_conv, state_shift, state_wkv, norm_mix_pre, norm_mix_post, norm_ffn_pre,
           norm_ffn_post, w_in, conv_w, shift_mu, w_decay2, decay_w0, w_a2, a0, w_g2, k_k, k_a, r_k, gn_gain,
           gn_bias, w_out, w_ff1, w_ff2):
    global _NC
    f = lambda a: np.ascontiguousarray(np.asarray(a, dtype=np.float32))
    x_prompt, x_sample = f(x_prompt), f(x_sample)
    w_in0, w_out0, w_ff10, w_ff20 = f(w_in)[0], f(w_out)[0], f(w_ff1)[0], f(w_ff2)[0]
    mu = f(shift_mu)[0]
    fm = lambda v: np.ascontiguousarray(v.reshape(-1, 128).T)
    wda_full = np.concatenate([f(w_decay2)[0], f(w_a2)[0]], axis=0)
    wg_full = f(w_g2)[0]
    plist = [mu[0:512], mu[512:1024], mu[1024:1536], f(decay_w0)[0], f(a0)[0], f(k_k)[0], f(k_a)[0],
             f(r_k)[0].reshape(-1), f(gn_gain)[0], f(gn_bias)[0]]
    par_full = np.ascontiguousarray(np.stack([fm(p) for p in plist], axis=-1))
    mu_sh = np.ascontiguousarray(np.stack([mu[1536:1664], mu[1664:1792]], axis=-1))
    convw = np.ascontiguousarray(f(conv_w)[0].T.reshape(4, 128, 3).transpose(1, 0, 2))
    gpre = np.ascontiguousarray(np.concatenate([fm(f(norm_mix_pre)[0]), fm(f(norm_ffn_pre)[0])], axis=1))
    gpost = np.ascontiguousarray(np.broadcast_to(f(norm_mix_post)[0][None, :], (128, D)))
    gffn = np.ascontiguousarray(np.broadcast_to(f(norm_ffn_post)[0][None, :], (128, D)))
    cmask = _masks()
    sc, ssh, swk = f(state_conv)[0], f(state_shift)[0], f(state_wkv)[0]
    in_maps = []
    for c in range(8):
        b, j = c // 4, c % 4
        xh = x_prompt[b, SEG * j - 2:SEG * j] if j > 0 else np.zeros((2, D), np.float32)
        xp = np.zeros((3 * SEG, D), np.float32)
        if j > 0:
            xp[(3 - j) * SEG:] = x_prompt[b, 0:SEG * j]
        in_maps.append({
            "xpre": xp, "xseg": np.ascontiguousarray(x_prompt[b, SEG * j:SEG * (j + 1)]),
            "xhalo": np.ascontiguousarray(xh), "xs": x_sample[c],
            "sconv": np.ascontiguousarray(sc[c].T.reshape(4, 128, 2).transpose(1, 0, 2)),
            "sshift": fm(ssh[c]),
            "swkv": np.ascontiguousarray(swk[c].reshape(4, 2, 64, 64).transpose(1, 3, 0, 2).reshape(128, 4, 64)),
            "w_in": w_in0, "w_out": w_out0, "w_ff1": w_ff10, "w_ff2": w_ff20,
            "wda_full": wda_full, "wg_full": wg_full,
            "par_full": par_full, "mu_sh": mu_sh,
            "convw": convw, "gpre": gpre, "gpost": gpost, "gffn": gffn, "cmask": cmask,
        })
    if _NC is None:
        _NC = build_program()
    res = run_bass_kernel_spmd(_NC, in_maps, core_ids=list(range(8))).results
    y_prompt = np.zeros((2, SEQ, D), np.float32)
    y_sample = np.zeros((8, TS, D), np.float32)
    conv_p = np.zeros((1, 2, 2, 512), np.float32)
    shift_p = np.zeros((1, 2, 1792), np.float32)
    wkv_p = np.zeros((1, 2, 8, 64, 64), np.float32)
    conv_s = np.zeros((1, 8, 2, 512), np.float32)
    shift_s = np.zeros((1, 8, 1792), np.float32)
    wkv_s = np.zeros((1, 8, 8, 64, 64), np.float32)
    for c in range(8):
        r = res[c]
        b, j = c // 4, c % 4
        y_prompt[b, SEG * j:SEG * (j + 1)] = r["y_seg"]
        y_sample[c] = r["y_s"]
        if j == 3:
            conv_p[0, b] = r["ucv_p"].transpose(2, 1, 0).reshape(2, 512)
        conv_s[0, c] = r["ucv_s"].transpose(2, 1, 0).reshape(2, 512)
        if j == 3:
            zl = r["zl_p"]
            for q in range(4):
                for k in range(3):
                    shift_p[0, b, k * 512 + q * 128:k * 512 + (q + 1) * 128] = zl[:, q, k]
            shift_p[0, b, 1536:1664] = zl[:, 0, 3]
            shift_p[0, b, 1664:1792] = zl[:, 0, 4]
            wkv_p[0, b] = r["H_p"].reshape(2, 64, 4, 64).transpose(2, 0, 3, 1).reshape(8, 64, 64)
        zs_ = r["zl_s"]
        for q in range(4):
            for k in range(3):
                shift_s[0, c, k * 512 + q * 128:k * 512 + (q + 1) * 128] = zs_[:, q, k]
        shift_s[0, c, 1536:1664] = zs_[:, 0, 3]
        shift_s[0, c, 1664:1792] = zs_[:, 0, 4]
        wkv_s[0, c] = r["H_s"].reshape(2, 64, 4, 64).transpose(2, 0, 3, 1).reshape(8, 64, 64)
    return (y_prompt, y_sample, conv_p, shift_p, wkv_p, conv_s, shift_s, wkv_s)
```

```python
from contextlib import ExitStack
import numpy as np
import ml_dtypes
import concourse.bass as bass
import concourse.mybir as mybir
from concourse.bass_utils import run_bass_kernel_spmd

F32 = mybir.dt.float32
BF16 = mybir.dt.bfloat16
I32 = mybir.dt.int32
AF = mybir.ActivationFunctionType
ALU = mybir.AluOpType

D = 1024
SEQ = 8192
SEG = 2048
TS = 32
NPROJ = 3328
DFF = 4096
NEPS = 1e-6
GEPS = 64e-5
DSC = float(np.exp(-0.5))
TG = 256
CH = 128


class Buf:
    def __init__(self, name, multi=False):
        self.name = name
        self.w = []
        self.r = []
        self.multi = multi
        self.sem = None
        self.cnt = 0
        self.bank = None


class Op:
    __slots__ = ("eng", "fn", "deps", "bdeps", "dma", "sig", "need", "idx", "cc")


class Prog:
    def __init__(self, nc, es):
        self.nc = nc
        self.es = es
        self.ops = []
        self.last = {}
        self.nsem = 0
        self.bank_last = {}
        self.cap = None

    def newsem(self, name):
        self.nsem += 1
        return self.es.enter_context(self.nc.semaphore(f"{name}_{self.nsem}"))

    def merge(self, A, B):
        ia = ib = 0
        na, nb = len(A), len(B)
        while ia < na or ib < nb:
            if ib >= nb or (ia < na and ia * nb <= ib * na):
                self.add(*A[ia])
                ia += 1
            else:
                self.add(*B[ib])
                ib += 1

    def add(self, eng, fn, R=(), W=(), dma=None, cc=False):
        if self.cap is not None:
            self.cap.append((eng, fn, list(R), list(W), dma, cc))
            return -1
        o = Op()
        o.cc = cc
        o.eng, o.fn, o.dma, o.sig, o.need = eng, fn, dma, None, False
        o.idx = len(self.ops)
        deps = set()
        for b in R:
            deps.update(b.w)
        for b in W:
            deps.update(b.r)
            if not b.multi:
                deps.update(b.w)
        for b in R:
            b.r.append(o.idx)
        for b in W:
            if b.multi:
                b.w.append(o.idx)
            else:
                b.w = [o.idx]
                b.r = []
        o.deps = deps
        o.bdeps = set()
        for b in list(R) + list(W):
            if b.bank is not None:
                if b.bank in self.bank_last:
                    o.bdeps.add(self.bank_last[b.bank])
        for b in list(R) + list(W):
            if b.bank is not None:
                self.bank_last[b.bank] = o.idx
        self.ops.append(o)
        self.last[eng] = o.idx
        return o.idx

    def barrier(self):
        lasts = set(self.last.values())
        for eng in ("pe", "act", "dve", "pool", "sp"):
            i = self.add(eng, lambda e: e.nop())
            self.ops[i].deps = set(lasts)
        lasts = set(self.last.values())
        for eng in ("pe", "act", "dve", "pool", "sp"):
            i = self.add(eng, lambda e: e.nop())
            self.ops[i].deps = set(lasts)

    def finalize(self, final_dma_ops):
        ops = self.ops
        fin = self.add("sp", lambda e: e.nop())
        ops[fin].deps = set(final_dma_ops)
        for o in ops:
            if o.dma is not None:
                o.need = True
        for o in ops:
            keep = set()
            for d in o.deps:
                p = ops[d]
                if p.eng == o.eng and p.dma is None and o.dma is None and o.eng == "pe":
                    continue
                keep.add(d)
                p.need = True
            for d in o.bdeps:
                p = ops[d]
                if p.eng != o.eng:
                    keep.add(d)
                    p.need = True
            o.deps = keep
        esem = {e: self.newsem("e" + e) for e in ("pe", "act", "dve", "pool", "sp")}
        ecnt = {e: 0 for e in esem}
        for o in ops:
            if not o.need:
                continue
            if o.dma is not None:
                b = o.dma
                if b.sem is None:
                    b.sem = self.newsem("d")
                if o.cc:
                    b.cnt += 1
                    o.sig = (b.sem, b.cnt, None)
                else:
                    b.cnt += 16
                    o.sig = (b.sem, b.cnt, 16)
            else:
                ecnt[o.eng] += 1
                o.sig = (esem[o.eng], ecnt[o.eng], 1)
        import os as _o2
        if _o2.environ.get("KDBG_PRINT"):
            print("SEMCOUNTS", ecnt, "nops", len(ops), {e: sum(1 for o in ops if o.eng == e) for e in ecnt}, flush=True)

    def emit(self, block):
        ops = self.ops
        streams = {e: [o for o in ops if o.eng == e] for e in ("pe", "act", "dve", "pool", "sp")}

        def run(engobj, lst):
            waited = {}
            for o in lst:
                need = {}
                for d in o.deps:
                    s, v, _ = ops[d].sig
                    k = id(s)
                    if waited.get(k, 0) < v and need.get(k, (None, 0))[1] < v:
                        need[k] = (s, v)
                for k, (s, v) in need.items():
                    engobj.wait_ge(s, v)
                    waited[k] = v
                ins = o.fn(engobj)
                if o.sig is not None:
                    if o.sig[2] is None:
                        ins.then_inc(o.sig[0])
                    else:
                        ins.then_inc(o.sig[0], o.sig[2])

        @block.tensor
        def _(e):
            run(e, streams["pe"])

        @block.scalar
        def _(e):
            run(e, streams["act"])

        @block.vector
        def _(e):
            run(e, streams["dve"])

        @block.gpsimd
        def _(e):
            run(e, streams["pool"])

        @block.sync
        def _(e):
            run(e, streams["sp"])


def build_program():
    nc = bass.Bass("TRN2", target_bir_lowering=False)
    es = ExitStack()
    P = Prog(nc, es)

    def din(name, shape, dt=F32):
        return nc.dram_tensor(name, list(shape), dt, kind="ExternalInput").ap()

    def dout(name, shape, dt=F32):
        return nc.dram_tensor(name, list(shape), dt, kind="ExternalOutput").ap()

    xpre = din("xpre", [3 * SEG, D])
    xseg = din("xseg", [SEG, D])
    xhalo = din("xhalo", [2, D])
    xs_in = din("xs", [TS, D])
    sconv = din("sconv", [128, 4, 2])
    sshift = din("sshift", [128, 14])
    swkv = din("swkv", [128, 4, 64])
    w_in = din("w_in", [D, NPROJ])
    w_out = din("w_out", [D, D])
    w_ff1 = din("w_ff1", [D, DFF])
    w_ff2 = din("w_ff2", [DFF, D])
    wda_full = din("wda_full", [128, 512])
    wg_full = din("wg_full", [128, 512])
    par_full = din("par_full", [128, 4, 10])
    mu_sh = din("mu_sh", [128, 2])
    convw = din("convw", [128, 4, 3])
    gpre = din("gpre", [128, 16])
    gpost_d = din("gpost", [128, D])
    gffn_d = din("gffn", [128, D])
    cmask = din("cmask", [128, 7, 128])

    y_seg = dout("y_seg", [SEG, D])
    y_s = dout("y_s", [TS, D])
    ucv_p = dout("ucv_p", [128, 4, 2])
    ucv_s = dout("ucv_s", [128, 4, 2])
    zl_p = dout("zl_p", [128, 4, 5])
    zl_s = dout("zl_s", [128, 4, 5])
    H_p = dout("H_p", [128, 4, 64])
    H_s = dout("H_s", [128, 4, 64])

    x1scr = nc.dram_tensor("x1scr", [SEG + TS, D], F32).ap()
    x1scr_b = Buf("x1scr", multi=True)

    out_dma_ops = []
    import os as _os
    KRSTOP = int(_os.environ.get("KDBG_RSTOP", "99"))

    class Stop(Exception):
        pass

    def rchk(n):
        if KRSTOP == n:
            raise Stop()

    def sb(stack, name, shape, dt=F32):
        t = stack.enter_context(nc.sbuf_tensor(name, list(shape), dt))
        return t, Buf(name)

    def pst(name, shape, dt=F32):
        t = es.enter_context(nc.psum_tensor(name, list(shape), dt))
        return t

    def mm(out, lhsT, rhs, R, W, start=True, stop=True):
        P.add("pe", lambda e: e.matmul(out, lhsT=lhsT, rhs=rhs, start=start, stop=stop), R, W)

    def tr(out, in_, ident, R, W):
        P.add("pe", lambda e: e.transpose(out, in_, ident), R, W)

    def act(out, in_, func, R, W, bias=None, scale=None, accum=None):
        kw = {}
        if bias is not None:
            kw["bias"] = bias
        if scale is not None:
            kw["scale"] = scale
        if accum is not None:
            kw["accum_out"] = accum
        P.add("act", lambda e: e.activation(out=out, in_=in_, func=func, **kw), R, W)

    def ts(out, in0, s1, s2, op0, op1, R, W, eng="dve"):
        if op1 is None:
            P.add(eng, lambda e: e.tensor_scalar(out=out, in0=in0, scalar1=s1, scalar2=None, op0=op0), R, W)
        else:
            P.add(eng, lambda e: e.tensor_scalar(out=out, in0=in0, scalar1=s1, scalar2=s2, op0=op0, op1=op1), R, W)

    def tt(out, in0, in1, op, R, W, eng="dve"):
        P.add(eng, lambda e: e.tensor_tensor(out=out, in0=in0, in1=in1, op=op), R, W)

    def stt(out, in0, scalar, in1, op0, op1, R, W):
        P.add("dve", lambda e: e.scalar_tensor_tensor(out=out, in0=in0, scalar=scalar, in1=in1, op0=op0, op1=op1), R, W)

    def cp(out, in_, R, W, eng="dve"):
        if eng == "act":
            act(out, in_, AF.Copy, R, W)
        else:
            P.add(eng, lambda e: e.tensor_copy(out=out, in_=in_), R, W)

    def recip(out, in_, R, W):
        P.add("dve", lambda e: e.reciprocal(out=out, in_=in_), R, W)

    def dma(out, in_, R, W, prim, q="sp"):
        return P.add(q, lambda e: e.dma_start(out=out, in_=in_), R, W, dma=prim)

    cm, cm_b = sb(es, "cm", [128, 7, 128])
    cmb, cmb_b = sb(es, "cmb", [128, 128], BF16)
    gpre_t, gpre_b = sb(es, "gpre_t", [128, 16])
    gpost, gpost_b = sb(es, "gpost_t", [128, D])
    gffn, gffn_b = sb(es, "gffn_t", [128, D])
    parf, parf_b = sb(es, "parf", [128, 4, 12])
    mush, mush_b = sb(es, "mush", [128, 2])
    cvw, cvw_b = sb(es, "cvw", [128, 4, 3])
    mu5all, mu5_b = sb(es, "mu5all", [128, 4, 5])
    xt = [sb(es, f"xt{i}", [128, D]) for i in range(2)]
    xn, xn_b = sb(es, "xn", [128, D], BF16)
    junk, junk_b = sb(es, "junk", [128, D], BF16)
    hT, hT_b = sb(es, "hT", [128, 8, 512], BF16)
    t1k, t1k_b = sb(es, "t1k", [128, D])
    x1t, x1t_b = sb(es, "x1t", [128, D])
    sm, sm_b = sb(es, "sm", [128, 8])
    sm2, sm2_b = sb(es, "sm2", [128, 8])

    IDF = cm[:, 0, :]
    MUS = cm[:, 1, :]
    MUI = cm[:, 2, :]
    MUSN = cm[:, 3, :]
    MLS = cm[:, 4, :]
    BONES = cm[:, 5, :]
    ONES = cm[:, 6, :]

    ps_tr = pst("ps_tr", [128, 1024], BF16)
    ps_tr_b = Buf("ps_tr")
    ps_tr_b.bank = 0
    ps_tfb = pst("ps_tfb", [128, 512])
    ps_tf = ps_tfb[:, 0:256]
    ps_tf_b = Buf("ps_tf")
    ps_tf_b.bank = 1
    ps_t2 = ps_tfb[:, 256:512].bitcast(BF16)
    ps_t2_b = Buf("ps_t2")
    ps_t2_b.bank = 1
    ps_z = [pst(f"ps_z{i}", [128, 512]) for i in range(2)]
    ps_z_b = [Buf(f"ps_z{i}") for i in range(2)]
    ps_c = [pst(f"ps_c{i}", [128, 512]) for i in range(4)]
    ps_c_b = [[Buf(f"ps_c{i}_{j}") for j in range(4)] for i in range(4)]
    for i in range(2):
        ps_z_b[i].bank = 2 + i
    for i in range(4):
        for j in range(4):
            ps_c_b[i][j].bank = 4 + i
    zrot = [0]

    def next_z():
        zrot[0] ^= 1
        return ps_z[zrot[0]], ps_z_b[zrot[0]]

    dma(cm[:], cmask, [], [cm_b], cm_b)
    cp(cmb[:], cm[:, 0, :], [cm_b], [cmb_b])
    dma(gpre_t[:], gpre, [], [gpre_b], gpre_b)
    dma(gpost[:], gpost_d, [], [gpost_b], gpost_b)
    dma(gffn[:], gffn_d, [], [gffn_b], gffn_b)
    dma(parf[:, :, 0:10], par_full, [], [parf_b], parf_b)
    dma(mush[:], mu_sh, [], [mush_b], mush_b)
    dma(cvw[:], convw, [], [cvw_b], cvw_b)
    for q in range(4):
        ts(parf[:, q, 10:12], parf[:, q, 3:5], -1.0, None, ALU.mult, None, [parf_b], [parf_b])
        cp(mu5all[:, q, 0:3], parf[:, q, 0:3], [parf_b], [mu5_b])
        cp(mu5all[:, q, 3:5], mush[:, 0:2], [mush_b], [mu5_b])

    esA = ExitStack()
    stg = [sb(esA, f"stg{i}", [128, 1024]) for i in range(2)]
    srot = [0]

    def load_cast(dst_ap, dst_b, src_ap, ncols, scale_ap=None, scale_b=None, stg_=None):
        st_ = stg_ if stg_ is not None else stg
        srot[0] = (srot[0] + 1) % len(st_)
        s_t, s_b = st_[srot[0]]
        dma(s_t[:, 0:ncols], src_ap, [], [s_b], s_b)
        if scale_ap is None:
            if srot[0] % 2:
                cp(dst_ap, s_t[:, 0:ncols], [s_b], [dst_b], eng="act")
            else:
                cp(dst_ap, s_t[:, 0:ncols], [s_b], [dst_b])
        else:
            if srot[0] % 2:
                act(dst_ap, s_t[:, 0:ncols], AF.Copy, [s_b, scale_b], [dst_b], scale=scale_ap)
            else:
                ts(dst_ap, s_t[:, 0:ncols], scale_ap, None, ALU.mult, None, [s_b, scale_b], [dst_b])

    wcv, wcv_b = sb(esA, "wcv", [128, 8, 1536], BF16)
    wo, wo_b = sb(esA, "wo", [128, 8, D], BF16)
    wdaf, wdaf_b = sb(esA, "wdaf", [128, 512], BF16)
    wgf, wgf_b = sb(esA, "wgf", [128, 512], BF16)
    TGM = TG
    zr, zr_b = sb(esA, "zr", [128, 5, TGM + 1])
    zs, zs_b = sb(esA, "zs", [128, 5, TGM])
    W_ = {}
    for nm in ("tmp", "tmp2", "sgw", "aa", "kk", "kkn", "ka", "k2", "pp", "Gs", "enG", "eGx", "yn"):
        W_[nm] = sb(esA, "w_" + nm, [128, TGM])
    E_OUT = []
    for s_ in range(2):
        E_OUT.append(dict(
            QR=sb(esA, f"eo{s_}_QR", [128, TGM // CH if TGM >= CH else 1, 2 * CH], BF16),
            PTt=sb(esA, f"eo{s_}_PT", [128, TGM], BF16), KTt=sb(esA, f"eo{s_}_KT", [128, TGM], BF16),
            VTt=sb(esA, f"eo{s_}_VT", [128, TGM], BF16), eG=sb(esA, f"eo{s_}_eG", [128, TGM]),
            bv=sb(esA, f"eo{s_}_bv", [128, TGM]), gate=sb(esA, f"eo{s_}_gate", [128, TGM])))
    lwa, lwa_b = sb(esA, "lwa", [128, TGM], BF16)
    sg, sg_b = sb(esA, "sg", [128, TGM], BF16)
    RfT, RfT_b = sb(esA, "RfT", [128, TGM], BF16)
    NCI = max(1, TG // CH)
    TOKs = [sb(esA, f"TOK{i}", [128, 4, 128], BF16) for i in range(NCI)]
    ONs = [sb(esA, f"ON{i}", [128, 128]) for i in range(NCI)]
    MTps = [sb(esA, f"MTp{i}", [128, 128], BF16) for i in range(NCI)]
    RfT_cb = [Buf(f"RfT{i}") for i in range(NCI)]
    Hbf4, _ = sb(esA, "Hbf4", [128, 4, 64], BF16)
    Hf4, _ = sb(esA, "Hf4", [128, 4, 64])
    Hbf4_b = [Buf(f"Hbf4_{q}") for q in range(4)]
    Hf4_b = [Buf(f"Hf4_{q}") for q in range(4)]
    zcar4 = [sb(esA, f"zcar{q}", [128, 5, 1]) for q in range(4)]
    Hs4, Hs4_b = sb(esA, "Hs4", [128, 4, 64])
    chn = []
    for ci_ in range(NCI):
        row = []
        for h in range(2):
            c_ = {}
            for nm in ("AqpT", "ArpT", "AqkTn", "ArkT", "A", "QYa", "QYb"):
                c_[nm] = sb(esA, f"c{ci_}{h}_{nm}", [128, 128], BF16)
            for nm in ("Pa", "Pb"):
                t_, b_ = sb(esA, f"c{ci_}{h}_{nm}", [128, 256], BF16)
                c_[nm] = (t_[:, 0:128], b_)
                c_[nm + "T"] = (t_[:, 128:256], b_)
                c_["PP" + nm[1]] = (t_, b_)
            c_["st"] = sb(esA, f"c{ci_}{h}_st", [128, 8])
            c_["mv"] = sb(esA, f"c{ci_}{h}_mv", [128, 4])
            row.append(c_)
        chn.append(row)
    for i_ in range(NCI):
        P.add("dve", lambda e, i_=i_: e.memset(MTps[i_][0][:], 0.0), [], [MTps[i_][1]])

    ucv, ucv_b = sb(esA, "ucv", [128, 4, 514])
    zhs, zhs_b = sb(esA, "zhs", [128, 512])
    yct, yct_b = sb(esA, "yct", [128, 512])
    ycv, ycv_b = sb(esA, "ycv", [128, 4, 512], BF16)
    ysm, ysm_b = sb(esA, "ysm", [128, 4, TS], BF16)
    ygt, ygt_b = sb(esA, "ygt", [128, 4, SEG], BF16)

    esS = ExitStack()
    wrw, wrw_b = sb(esS, "wrw", [128, 8, 1792], BF16)

    wcv_kb = [Buf(f"wcv{k}") for k in range(8)]
    wrw_kb = [Buf(f"wrw{k}") for k in range(8)]
    wo_kb = [Buf(f"wo{k}") for k in range(8)]
    wrw_kb2 = [Buf(f"wrwb{k}") for k in range(8)]
    wcv_kb2 = [Buf(f"wcvb{k}") for k in range(8)]
    for kc in range(8):
        load_cast(wrw[:, kc, 0:896], wrw_kb[kc], w_in[kc * 128:(kc + 1) * 128, 1536:1536 + 896], 896,
                  gpre_t[:, kc:kc + 1], gpre_b)
        load_cast(wrw[:, kc, 896:1792], wrw_kb2[kc], w_in[kc * 128:(kc + 1) * 128, 1536 + 896:3328], 896,
                  gpre_t[:, kc:kc + 1], gpre_b)
    for kc in range(8):
        load_cast(wcv[:, kc, 0:768], wcv_kb[kc], w_in[kc * 128:(kc + 1) * 128, 0:768], 768,
                  gpre_t[:, kc:kc + 1], gpre_b)
        load_cast(wcv[:, kc, 768:1536], wcv_kb2[kc], w_in[kc * 128:(kc + 1) * 128, 768:1536], 768,
                  gpre_t[:, kc:kc + 1], gpre_b)
    for kc in range(8):
        load_cast(wo[:, kc, :], wo_kb[kc], w_out[kc * 128:(kc + 1) * 128, :], D)
    load_cast(wdaf[:], wdaf_b, wda_full, 512)
    load_cast(wgf[:], wgf_b, wg_full, 512)

    def rstd_from(acc_ap, n_part, scale, eps, R_b):
        ts(sm[:n_part, 6:7], acc_ap, scale, eps, ALU.mult, ALU.add, [R_b], [sm_b])
        act(sm[:n_part, 6:7], sm[:n_part, 6:7], AF.Ln, [sm_b], [sm_b])
        act(sm[:n_part, 7:8], sm[:n_part, 6:7], AF.Exp, [sm_b], [sm_b], scale=-0.5)
        return sm[:n_part, 7:8]

    def norm_transpose(x_ap_fn, ntiles, tn, dstT, dstT_b):
        for ti in range(ntiles):
            x_t, x_b = xt[ti % 2]
            dma(x_t[:tn, :], x_ap_fn(ti), [], [x_b], x_b)
            act(junk[:tn, :], x_t[:tn, :], AF.Square, [x_b], [junk_b, sm_b], accum=sm[:tn, 0:1])
            r = rstd_from(sm[:tn, 0:1], tn, 1.0 / D, NEPS, sm_b)
            act(xn[:tn, :], x_t[:tn, :], AF.Copy, [x_b, sm_b], [xn_b], scale=r)
            for kc in range(8):
                tr(ps_tr[:, kc * 128:kc * 128 + tn], xn[:tn, kc * 128:(kc + 1) * 128], cmb[:tn, :tn],
                   [xn_b, cmb_b], [ps_tr_b])
            src = ps_tr[:, :].rearrange("p (k t) -> p k t", k=8)[:, :, 0:tn]
            cp(dstT[:, :, ti * 128:ti * 128 + tn], src, [ps_tr_b], [dstT_b], eng=("act" if ti % 2 else "dve"))

    def rwkv_pair(T, wcol, wcol_b, par, par_b, wda_ap, wda_b, wg_ap, wg_b, init_prev, H_init, y_dst, y_dst_b,
                  first, last_out, Hbf=None, Hbf_b=None, Hf=None, Hf_b=None, zcar=None, zcar_b=None, so=False, full_proj=False, eset=0, part="EC", mu5=None):
        C = min(CH, T)
        nch = T // C
        L = int(np.log2(C)) - 1
        tmp, tmp_b = W_["tmp"]
        tmp2, tmp2_b = W_["tmp2"]
        sgw, sgw_b = W_["sgw"]
        aa, aa_b = W_["aa"]
        gate, gate_b = E_OUT[eset]["gate"]
        QR, QR_b = E_OUT[eset]["QR"]
        PTt, PT_b = E_OUT[eset]["PTt"]
        KTt, KT_b = E_OUT[eset]["KTt"]
        VTt, VT_b = E_OUT[eset]["VTt"]
        kk, kk_b = W_["kk"]
        kkn, kkn_b = W_["kkn"]
        ka, ka_b = W_["ka"]
        k2, k2_b = W_["k2"]
        pp, pp_b = W_["pp"]
        bv, bv_b = E_OUT[eset]["bv"]
        Gs, Gs_b = W_["Gs"]
        eG, eG_b = E_OUT[eset]["eG"]
        enG, enG_b = W_["enG"]
        eGx, eGx_b = W_["eGx"]
        yn, yn_b = W_["yn"]
        if "E" in part:
            _rw_E(locals())
        if "C" in part:
            _rw_C(locals())

    def _rw_E(L_):
        globals_ = L_
        (T, C, nch, L, so, full_proj, first, par, par_b, wcol, wcol_b, wda_ap, wda_b, wg_ap, wg_b, init_prev, H_init,
         zcar, zcar_b) = [L_[k] for k in ("T", "C", "nch", "L", "so", "full_proj", "first", "par", "par_b", "wcol", "wcol_b",
                                          "wda_ap", "wda_b", "wg_ap", "wg_b", "init_prev", "H_init", "zcar", "zcar_b")]
        mu5 = L_["mu5"]
        (tmp, tmp_b, tmp2, tmp2_b, sgw, sgw_b, aa, aa_b, gate, gate_b, QR, QR_b, PTt, PT_b, KTt, KT_b, VTt, VT_b, kk, kk_b,
         kkn, kkn_b, ka, ka_b, k2, k2_b, pp, pp_b, bv, bv_b, Gs, Gs_b, eG, eG_b, enG, enG_b, eGx, eGx_b) = [L_[k] for k in (
            "tmp", "tmp_b", "tmp2", "tmp2_b", "sgw", "sgw_b", "aa", "aa_b", "gate", "gate_b", "QR", "QR_b", "PTt", "PT_b",
            "KTt", "KT_b", "VTt", "VT_b", "kk", "kk_b", "kkn", "kkn_b", "ka", "ka_b", "k2", "k2_b", "pp", "pp_b", "bv", "bv_b",
            "Gs", "Gs_b", "eG", "eG_b", "enG", "enG_b", "eGx", "eGx_b")]
        if first:
            init_prev()
            H_init()
        else:
            r0_, r1_ = (1, 4) if so else (0, 5)
            cp(zr[:, r0_:r1_, 0:1], zcar[:, r0_:r1_, 0:1], [zcar_b], [zr_b])
        CL = (1, 2, 3) if so else (0, 1, 2, 3, 4)
        CLp = (0, 1, 2, 3, 4) if (full_proj or not so) else CL
        for c in CLp:
            pz, pz_b = next_z()
            for kc in range(8):
                mm(pz[:, 0:T], wcol(c, kc), hT[:, kc, 0:T], wcol_b(kc) + [hT_b], [pz_b], start=(kc == 0), stop=(kc == 7))
            cp(zr[:, c, 1:T + 1], pz[:, 0:T], [pz_b], [zr_b], eng="act")
        if zcar is not None:
            w0_, w1_ = (0, 5) if (full_proj or not so) else (1, 4)
            cp(zcar[:, w0_:w1_, 0:1], zr[:, w0_:w1_, T:T + 1], [zr_b], [zcar_b])
        rchk(1)
        c0, c1 = (1, 4) if so else (0, 5)
        mub = mu5[:, c0:c1].unsqueeze(2).to_broadcast([128, c1 - c0, T])
        tt(zs[:, c0:c1, 0:T], zr[:, c0:c1, 0:T], zr[:, c0:c1, 1:T + 1], ALU.subtract, [zr_b], [zs_b])
        tt(zs[:, c0:c1, 0:T], zs[:, c0:c1, 0:T], mub, ALU.mult, [zs_b, mu5_b], [zs_b])
        tt(zs[:, c0:c1, 0:T], zs[:, c0:c1, 0:T], zr[:, c0:c1, 1:T + 1], ALU.add, [zs_b, zr_b], [zs_b])
        rchk(2)
        def sig_act(dst, src, R, W, pr, scale=-1.0, bias=None, wk=None, wk_b=None):
            act(wk[pr, 0:T], src, AF.Exp, R, [wk_b], scale=scale, bias=bias)
            act(wk[pr, 0:T], wk[pr, 0:T], AF.Ln, [wk_b, cm_b], [wk_b], bias=ONES[pr, 0:1])
            act(dst, wk[pr, 0:T], AF.Exp, [wk_b], W, scale=-1.0)
        p0 = slice(0, 64)
        pa_ = slice(0, 128)
        sig_act(tmp[p0, 0:T], zs[p0, 3, 0:T], [zs_b], [tmp_b], p0, scale=-2.0, wk=tmp, wk_b=tmp_b)
        ts(lwa[0:64, 0:T], tmp[0:64, 0:T], 2.0, -1.0, ALU.mult, ALU.add, [tmp_b], [lwa_b])
        cp(lwa[64:128, 0:T], zs[64:128, 3, 0:T], [zs_b], [lwa_b], eng="act")
        if not so:
            sig_act(sg[:, 0:T], zs[:, 4, 0:T], [zs_b], [sg_b], pa_, wk=tmp2, wk_b=tmp2_b)
        rchk(3)
        pz, pz_b = next_z()
        mm(pz[:, 0:T], wda_ap[0:64, :], lwa[0:64, 0:T], [wda_b, lwa_b], [pz_b])
        sig_act(sgw[:, 0:T], pz[:, 0:T], [pz_b, par_b], [sgw_b], pa_, bias=par[:, 10:11], wk=tmp, wk_b=tmp_b)
        pz, pz_b = next_z()
        mm(pz[:, 0:T], wda_ap[64:128, :], lwa[64:128, 0:T], [wda_b, lwa_b], [pz_b])
        sig_act(aa[:, 0:T], pz[:, 0:T], [pz_b, par_b], [aa_b], pa_, bias=par[:, 11:12], wk=tmp2, wk_b=tmp2_b)
        if not so:
            pz, pz_b = next_z()
            mm(pz[:, 0:T], wg_ap, sg[:, 0:T], [wg_b, sg_b], [pz_b])
            cp(gate[:, 0:T], pz[:, 0:T], [pz_b], [gate_b], eng="act")
        rchk(4)
        act(tmp[:, 0:T], zs[:, 1, 0:T], AF.Square, [zs_b, par_b], [tmp_b], scale=par[:, 5:6])
        pz, pz_b = next_z()
        mm(pz[:, 0:T], BONES, tmp[:, 0:T], [cm_b, tmp_b], [pz_b])
        ts(tmp2[:, 0:T], pz[:, 0:T], 1e-18, None, ALU.max, None, [pz_b], [tmp2_b])
        act(tmp2[:, 0:T], tmp2[:, 0:T], AF.Ln, [tmp2_b], [tmp2_b])
        act(tmp2[:, 0:T], tmp2[:, 0:T], AF.Exp, [tmp2_b], [tmp2_b], scale=-0.5)
        stt(kkn[:, 0:T], zs[:, 1, 0:T], par[:, 5:6], tmp2[:, 0:T], ALU.mult, ALU.mult, [zs_b, par_b, tmp2_b], [kkn_b])
        rchk(5)
        ts(ka[:, 0:T], aa[:, 0:T], -1.0, None, ALU.add, None, [aa_b], [ka_b])
        ts(ka[:, 0:T], ka[:, 0:T], par[:, 6:7], ONES[:, 0:1], ALU.mult, ALU.add, [ka_b, par_b, cm_b], [ka_b])
        tt(k2[:, 0:T], zs[:, 1, 0:T], ka[:, 0:T], ALU.mult, [zs_b, ka_b], [k2_b])
        tt(pp[:, 0:T], aa[:, 0:T], kkn[:, 0:T], ALU.mult, [aa_b, kkn_b], [pp_b])
        rchk(6)
        if not so:
            stt(tmp[:, 0:T], zs[:, 0, 0:T], par[:, 7:8], k2[:, 0:T], ALU.mult, ALU.mult, [zs_b, par_b, k2_b], [tmp_b])
            pz, pz_b = next_z()
            mm(pz[:, 0:T], BONES, tmp[:, 0:T], [cm_b, tmp_b], [pz_b])
            tt(bv[:, 0:T], pz[:, 0:T], zs[:, 2, 0:T], ALU.mult, [pz_b, zs_b], [bv_b])
        rchk(7)
        for ci in range(nch):
            cs = slice(ci * C, (ci + 1) * C)
            P.add("dve", lambda e, cs=cs: e.tensor_tensor_scan(out=Gs[:, cs], data0=ONES[:, 0:C], data1=sgw[:, cs],
                                                               initial=0.0, op0=ALU.mult, op1=ALU.add),
                  [cm_b, sgw_b], [Gs_b])
        tt(tmp[:, 0:T], Gs[:, 0:T], sgw[:, 0:T], ALU.subtract, [Gs_b, sgw_b], [tmp_b])
        act(eG[:, 0:T], Gs[:, 0:T], AF.Exp, [Gs_b], [eG_b], scale=-DSC)
        act(enG[:, 0:T], Gs[:, 0:T], AF.Exp, [Gs_b], [enG_b], scale=DSC)
        act(eGx[:, 0:T], tmp[:, 0:T], AF.Exp, [tmp_b], [eGx_b], scale=-DSC)
        rchk(8)
        v3 = lambda ap: ap.rearrange("p (n c) -> p n c", c=C)
        tt(QR[:, 0:nch, 0:C], v3(kkn[:, 0:T]), v3(eGx[:, 0:T]), ALU.mult, [kkn_b, eGx_b], [QR_b])
        if not so:
            tt(QR[:, 0:nch, C:2 * C], v3(zs[:, 0, 0:T]), v3(eG[:, 0:T]), ALU.mult, [zs_b, eG_b], [QR_b])
        tt(PTt[:, 0:T], pp[:, 0:T], enG[:, 0:T], ALU.mult, [pp_b, enG_b], [PT_b])
        tt(KTt[:, 0:T], k2[:, 0:T], enG[:, 0:T], ALU.mult, [k2_b, enG_b], [KT_b])
        cp(VTt[:, 0:T], zs[:, 2, 0:T], [zs_b], [VT_b], eng="act")

        rchk(9)

    def _rw_C(L_):
        (T, C, nch, L, so, last_out, par, par_b, y_dst, y_dst_b, Hbf, Hbf_b, Hf, Hf_b) = [L_[k] for k in (
            "T", "C", "nch", "L", "so", "last_out", "par", "par_b", "y_dst", "y_dst_b", "Hbf", "Hbf_b", "Hf", "Hf_b")]
        (gate, gate_b, QR, QR_b, PTt, PT_b, KTt, KT_b, VTt, VT_b, bv, bv_b, eG, eG_b, yn, yn_b) = [L_[k] for k in (
            "gate", "gate_b", "QR", "QR_b", "PTt", "PT_b", "KTt", "KT_b", "VTt", "VT_b", "bv", "bv_b", "eG", "eG_b", "yn", "yn_b")]
        chains = [(ci, h) for ci in range(nch) for h in range(2)]

        def bank(ci, h):
            return ps_c[2 * ci + h], ps_c_b[2 * ci + h]
        CS = [slice(ci * C, (ci + 1) * C) for ci in range(nch)]
        NQ = C if so else 2 * C
        for ci in range(nch):
            cs = CS[ci]
            TOK, TOK_b = TOKs[ci]
            srcs = [(QR[:, ci, 0:C], QR_b), (PTt[:, cs], PT_b), (KTt[:, cs], KT_b), (VTt[:, cs], VT_b)]
            for q_, (s_ap, s_b) in enumerate(srcs):
                tr(ps_t2[:C, q_ * 128:(q_ + 1) * 128], s_ap, cmb[:, :], [s_b, cmb_b], [ps_t2_b])
            cp(TOK[:C, 1:4, :], ps_t2[:C, 128:512].rearrange("p (q c) -> p q c", q=3), [ps_t2_b], [TOK_b])
            for h in range(2):
                QY0, QY0_b = chn[ci][h]["QYa"]
                qc = slice(0, 64) if h == 0 else slice(64, 128)
                cp(QY0[:C, qc], ps_t2[:C, 64 * h:64 * h + 64], [ps_t2_b], [QY0_b], eng="act")
        rchk(10)
        for ci, h in chains:
            c_ = chn[ci][h]
            cs = CS[ci]
            sl = slice(64 * h, 64 * h + 64)
            bk, bb = bank(ci, h)
            mm(bk[:C, 0:NQ], PTt[sl, cs], QR[sl, ci, 0:NQ], [PT_b, QR_b], [bb[0], bb[1]])
            mm(bk[:C, 256:256 + NQ], KTt[sl, cs], QR[sl, ci, 0:NQ], [KT_b, QR_b], [bb[2], bb[3]])
            tt(c_["AqpT"][0][:C, :C], bk[:C, 0:C], MUS[:C, :C], ALU.mult, [bb[0], bb[1], cm_b], [c_["AqpT"][1]])
            if not so:
                tt(c_["ArpT"][0][:C, :C], bk[:C, C:2 * C], MUI[:C, :C], ALU.mult, [bb[0], bb[1], cm_b], [c_["ArpT"][1]])
            tt(c_["AqkTn"][0][:C, :C], bk[:C, 256:256 + C], MUSN[:C, :C], ALU.mult, [bb[2], bb[3], cm_b], [c_["AqkTn"][1]])
            if not so:
                tt(c_["ArkT"][0][:C, :C], bk[:C, 256 + C:256 + 2 * C], MUI[:C, :C], ALU.mult, [bb[2], bb[3], cm_b], [c_["ArkT"][1]])
        for ci, h in chains:
            c_ = chn[ci][h]
            cs = CS[ci]
            sl = slice(64 * h, 64 * h + 64)
            bk, bb = bank(ci, h)
            mm(bk[:C, 0:C], QR[sl, ci, 0:C], PTt[sl, cs], [QR_b, PT_b], [bb[0]])
            tt(c_["A"][0][:C, :C], bk[:C, 0:C], MLS[:C, :C], ALU.mult, [bb[0], cm_b], [c_["A"][1]])
        rchk(11)
        for ci, h in chains:
            c_ = chn[ci][h]
            sl = slice(64 * h, 64 * h + 64)
            bk, bb = bank(ci, h)
            TOK, TOK_b = TOKs[ci]
            yc_ = slice(64, 128) if h == 0 else slice(0, 64)
            QY0, QY0_b = c_["QYa"]
            mm(bk[:C, 384:448], c_["AqkTn"][0][:C, :C], TOK[:C, 3, sl], [c_["AqkTn"][1], TOK_b], [bb[3]])
            cp(QY0[:C, yc_], bk[:C, 384:448], [bb[3]], [QY0_b], eng="act")
        rchk(12)
        cur = {}
        for ci, h in chains:
            c_ = chn[ci][h]
            bk, bb = bank(ci, h)
            mm(bk[:C, 256:384], c_["AqpT"][0][:C, :C], c_["QYa"][0][:C, :], [c_["AqpT"][1], c_["QYa"][1]], [bb[2]])
            tt(c_["QYb"][0][:C, :], c_["QYa"][0][:C, :], bk[:C, 256:384], ALU.subtract, [c_["QYa"][1], bb[2]], [c_["QYb"][1]])
            cur[(ci, h)] = "QYb"
        rchk(13)
        pcur = ("A", "AqpT")
        for j in range(1, L + 1):
            pn = ("Pa", "PaT") if j % 2 == 1 else ("Pb", "PbT")
            Pp, PpT = pcur
            for k_, (ci, h) in enumerate(chains):
                c_ = chn[ci][h]
                bk, bb = bank(ci, h)
                if j < L:
                    mm(bk[:C, 0:C], c_[PpT][0][:C, :C], c_[Pp][0][:C, :C], [c_[PpT][1], c_[Pp][1]], [bb[0]])
                mm(bk[:C, 128:128 + C], c_[Pp][0][:C, :C], c_[PpT][0][:C, :C], [c_[PpT][1], c_[Pp][1]], [bb[1]])
            for k_, (ci, h) in enumerate(chains):
                c_ = chn[ci][h]
                bk, bb = bank(ci, h)
                e1, e2 = ("act", "dve") if k_ % 2 == 0 else ("dve", "act")
                if j < L:
                    cp(c_["PP" + pn[0][1]][0][:C, 0:256], bk[:C, 0:256], [bb[0], bb[1]], [c_[pn[0]][1]], eng=e1)
                else:
                    cp(c_[pn[1]][0][:C, :C], bk[:C, 128:128 + C], [bb[1]], [c_[pn[1]][1]], eng=e2)
            for ci, h in chains:
                c_ = chn[ci][h]
                bk, bb = bank(ci, h)
                src = cur[(ci, h)]
                dst = "QYa" if src == "QYb" else "QYb"
                mm(bk[:C, 256:384], c_[pn[1]][0][:C, :C], c_[src][0][:C, :], [c_[pn[1]][1], c_[src][1]], [bb[2]])
                tt(c_[dst][0][:C, :], c_[src][0][:C, :], bk[:C, 256:384], ALU.add, [c_[src][1], bb[2]], [c_[dst][1]])
                cur[(ci, h)] = dst
            pcur = pn
        rchk(14)
        for ci in range(nch):
            cs = CS[ci]
            TOK, TOK_b = TOKs[ci]
            ON, ON_b = ONs[ci]
            MTp, MTp_b = MTps[ci]
            for h in range(2):
                c_ = chn[ci][h]
                sl = slice(64 * h, 64 * h + 64)
                bk, bb = bank(ci, h)
                WU, WU_b = c_[cur[(ci, h)]]
                if h == 0:
                    Wl, Ul, orow = WU[:C, 0:64], WU[:C, 64:128], slice(0, 64)
                else:
                    Wl, Ul, orow = WU[:C, 0:128], WU[:C, 0:64], slice(0, 128)
                if not so:
                    mm(bk[orow, 0:C], Wl, c_["ArpT"][0][:C, :C], [WU_b, c_["ArpT"][1]], [bb[0]])
                    tt(RfT[sl, cs], QR[sl, ci, C:2 * C], bk[sl, 0:C], ALU.subtract, [QR_b, bb[0]], [RfT_cb[ci]])
                mm(bk[orow, 128:192], Wl, TOK[:C, 1, sl], [WU_b, TOK_b], [bb[1]])
                tt(MTp[sl, sl], IDF[sl, sl], bk[sl, 128:192], ALU.subtract, [cm_b, bb[1]], [MTp_b])
                if not so:
                    mm(bk[:C, 384:448], c_["ArpT"][0][:C, :C], Ul, [c_["ArpT"][1], WU_b], [bb[3]], start=True, stop=False)
                    mm(bk[:C, 384:448], c_["ArkT"][0][:C, :C], TOK[:C, 3, sl], [c_["ArkT"][1], TOK_b], [bb[3]], start=False, stop=False)
                    mm(bk[:C, 384:448], RfT[sl, cs], Hbf[sl, :], [RfT_cb[ci], Hbf_b], [bb[3]], start=False, stop=True)
                    st_, st_b = c_["st"]
                    mv_, mv_b = c_["mv"]
                    P.add("dve", lambda e, bk=bk, st_=st_: e.bn_stats(out=st_[:C, 0:6], in_=bk[:C, 384:448]), [bb[3]], [st_b])
                    P.add("dve", lambda e, st_=st_, mv_=mv_: e.bn_aggr(out=mv_[:C, 0:2], in_=st_[:C, 0:6]), [st_b], [mv_b])
                    ts(mv_[:C, 2:3], mv_[:C, 1:2], GEPS, None, ALU.add, None, [mv_b], [mv_b])
                    act(mv_[:C, 2:3], mv_[:C, 2:3], AF.Ln, [mv_b], [mv_b])
                    act(mv_[:C, 3:4], mv_[:C, 2:3], AF.Exp, [mv_b], [mv_b], scale=-0.5)
                    ts(ON[:C, sl], bk[:C, 384:448], mv_[:C, 0:1], mv_[:C, 3:4], ALU.subtract, ALU.mult, [bb[3], mv_b], [ON_b])
                mm(bk[:, 256:320], TOK[:C, 1, :], Ul, [TOK_b, WU_b], [bb[2]], start=True, stop=False)
                mm(bk[:, 256:320], TOK[:C, 2, :], TOK[:C, 3, sl], [TOK_b], [bb[2]], start=False, stop=False)
                mm(bk[:, 256:320], MTp[sl, :], Hbf[sl, :], [MTp_b, Hbf_b], [bb[2]], start=False, stop=True)
                gl = eG[sl, ci * C + C - 1:ci * C + C]
                ts(Hbf[sl, :], bk[sl, 256:320], gl, None, ALU.mult, None, [bb[2], eG_b], [Hbf_b])
                if last_out and ci == nch - 1:
                    ts(Hf[sl, :], bk[sl, 256:320], gl, None, ALU.mult, None, [bb[2], eG_b], [Hf_b])
        rchk(15)
        if not so:
            for ci in range(nch):
                cs = CS[ci]
                ON, ON_b = ONs[ci]
                tr(ps_tf[:, 0:C], ON[:C, :], IDF[:C, :C], [ON_b, cm_b], [ps_tf_b])
                ts(yn[:, cs], ps_tf[:, 0:C], par[:, 8:9], par[:, 9:10], ALU.mult, ALU.add, [ps_tf_b, par_b], [yn_b])
            tt(yn[:, 0:T], yn[:, 0:T], bv[:, 0:T], ALU.add, [yn_b, bv_b], [yn_b])
            tt(y_dst, yn[:, 0:T], gate[:, 0:T], ALU.mult, [yn_b, gate_b], [y_dst_b])

    def conv_group(T, wb_list, first_init):
        if first_init is not None:
            first_init()
        else:
            cp(ucv[:, :, 0:2], ucv[:, :, 512:514], [ucv_b], [ucv_b])
        for cq in range(4):
            pzh, pzh_b = next_z()
            for kc in range(8):
                mm(pzh[:, 0:T], wcv[:, kc, (8 + cq) * 128:(9 + cq) * 128], hT[:, kc, 0:T], [wcv_kb[kc], wcv_kb2[kc], hT_b], [pzh_b],
                   start=(kc == 0), stop=(kc == 7))
            cp(zhs[:, 0:T], pzh[:, 0:T], [pzh_b], [zhs_b], eng="act")
            pzc, pzc_b = next_z()
            for kc in range(8):
                mm(pzc[:, 0:T], wcv[:, kc, (4 + cq) * 128:(5 + cq) * 128], hT[:, kc, 0:T], [wcv_kb[kc], wcv_kb2[kc], hT_b], [pzc_b],
                   start=(kc == 0), stop=(kc == 7))
            tt(ucv[:, cq, 2:2 + T], pzc[:, 0:T], zhs[:, 0:T], ALU.mult, [pzc_b, zhs_b], [ucv_b])
            ts(yct[:, 0:T], ucv[:, cq, 0:T], cvw[:, cq, 0:1], None, ALU.mult, None, [ucv_b, cvw_b], [yct_b])
            stt(yct[:, 0:T], ucv[:, cq, 1:1 + T], cvw[:, cq, 1:2], yct[:, 0:T], ALU.mult, ALU.add, [ucv_b, cvw_b, yct_b], [yct_b])
            stt(yct[:, 0:T], ucv[:, cq, 2:2 + T], cvw[:, cq, 2:3], yct[:, 0:T], ALU.mult, ALU.add, [ucv_b, cvw_b, yct_b], [yct_b])
            pzb, pzb_b = next_z()
            for kc in range(8):
                mm(pzb[:, 0:T], wcv[:, kc, cq * 128:(cq + 1) * 128], hT[:, kc, 0:T], [wcv_kb[kc], wcv_kb2[kc], hT_b], [pzb_b],
                   start=(kc == 0), stop=(kc == 7))
            tt(ycv[:, cq, 0:T], pzb[:, 0:T], yct[:, 0:T], ALU.mult, [pzb_b, yct_b], [ycv_b])

    wrot = [0]

    def wout_tile(tn, mix_fn, x_ap, x1_dst_row0):
        wrot[0] ^= 1
        bM = [ps_c[2 * wrot[0]], ps_c[2 * wrot[0] + 1]]
        bMb = [ps_c_b[2 * wrot[0]], ps_c_b[2 * wrot[0] + 1]]
        for half in range(2):
            for c in range(8):
                ap_, b_ = mix_fn(c)
                mm(bM[half][:tn, :], ap_, wo[:, c, half * 512:(half + 1) * 512], [b_, wo_kb[c]], bMb[half],
                   start=(c == 0), stop=(c == 7))
        x_t, x_b = xt[0]
        dma(x_t[:tn, :], x_ap, [], [x_b], x_b)
        for half in range(2):
            act(junk[:tn, 0:512], bM[half][:tn, :], AF.Square, bMb[half], [junk_b, sm2_b], accum=sm2[:tn, half:half + 1])
        tt(sm2[:tn, 2:3], sm2[:tn, 0:1], sm2[:tn, 1:2], ALU.add, [sm2_b], [sm2_b])
        r = rstd_from(sm2[:tn, 2:3], tn, 1.0 / D, NEPS, sm2_b)
        for half in range(2):
            hs = slice(half * 512, (half + 1) * 512)
            stt(t1k[:tn, hs], bM[half][:tn, :], r, gpost[:tn, hs], ALU.mult, ALU.mult, bMb[half] + [sm_b, gpost_b], [t1k_b])
        tt(x1t[:tn, :], t1k[:tn, :], x_t[:tn, :], ALU.add, [t1k_b, x_b], [x1t_b])
        dma(x1scr[x1_dst_row0:x1_dst_row0 + tn, :], x1t[:tn, :], [x1t_b], [x1scr_b], x1t_b)

    def _phases(chk, stacks):
        import os
        def pair_args(q):
            zcols = [q * 128, 512 + q * 128, 1024 + q * 128, 1536, 1664]

            def wcol(c, kc, zcols=zcols):
                return wrw[:, kc, zcols[c]:zcols[c] + 128]
            return dict(wcol=wcol, wcol_b=lambda kc: [wrw_kb[kc], wrw_kb2[kc]], par=parf[:, q, :], par_b=parf_b,
                        wda_ap=wdaf[:, q * 128:(q + 1) * 128], wda_b=wdaf_b, wg_ap=wgf[:, q * 128:(q + 1) * 128], wg_b=wgf_b,
                        Hbf=Hbf4[:, q, :], Hbf_b=Hbf4_b[q], Hf=Hf4[:, q, :], Hf_b=Hf4_b[q], mu5=mu5all[:, q, :])

        chk(1)
        NPRE = int(os.environ.get("KDBG_NPRE", str(3 * SEG // TG)))
        NOWN = int(os.environ.get("KDBG_NOWN", str(SEG // TG)))
        calls = [(g, q) for g in range(NPRE + NOWN) for q in range(4)]

        def call_kw(g, q):
            own = g >= NPRE
            zc_t, zc_b = zcar4[q]

            def init_prev():
                P.add("dve", lambda e: e.memset(zr[:, :, 0:1], 0.0), [], [zr_b])

            def H_init(q=q):
                P.add("dve", lambda e, q=q: e.memset(Hbf4[:, q, :], 0.0), [], [Hbf4_b[q]])
            ydst = ygt[:, q, (g - NPRE) * TG:(g - NPRE + 1) * TG] if own else None
            kw = dict(T=TG, init_prev=init_prev, H_init=H_init, y_dst=ydst, y_dst_b=ygt_b, first=(g == 0),
                      last_out=(g == NPRE + NOWN - 1), zcar=zc_t, zcar_b=zc_b, so=(not own), full_proj=(g == NPRE - 1))
            kw.update(pair_args(q))
            return kw

        def emit_norm(g):
            if g >= NPRE:
                r0 = (g - NPRE) * TG
                norm_transpose(lambda ti, r0=r0: xseg[r0 + ti * 128:r0 + (ti + 1) * 128, :], TG // 128, 128, hT, hT_b)
            else:
                norm_transpose(lambda ti, g=g: xpre[g * TG + ti * 128:g * TG + (ti + 1) * 128, :], TG // 128, 128, hT, hT_b)

        def emitE(n):
            g, q = calls[n]
            if q == 0:
                emit_norm(g)
            rwkv_pair(part="E", eset=n % 2, **call_kw(g, q))

        def emitC(n):
            g, q = calls[n]
            rwkv_pair(part="C", eset=n % 2, **call_kw(g, q))
        PIPE = os.environ.get("KDBG_NOPIPE") != "1"
        if len(calls) > 0:
            emitE(0)
        for n in range(len(calls)):
            if PIPE:
                P.cap = []
                emitC(n)
                A_ = P.cap
                P.cap = []
                if n + 1 < len(calls):
                    emitE(n + 1)
                B_ = P.cap
                P.cap = None
                P.merge(A_, B_)
            else:
                emitC(n)
                if n + 1 < len(calls):
                    emitE(n + 1)
        chk(2)
        for q in range(4):
            out_dma_ops.append(dma(H_p[:, q, :], Hf4[:, q, :], [Hf4_b[q]], [], Hf4_b[q]))
            out_dma_ops.append(dma(zl_p[:, q, :], zcar4[q][0][:, :, 0:1].rearrange("p c o -> p (c o)"), [zcar4[q][1]], [], zcar4[q][1]))
        chk(3)

        chk(0)
        norm_transpose(lambda ti: xs_in[:, :], 1, TS, hT, hT_b)
        chk(10)

        def s_conv_init():
            dma(ucv[:, :, 0:2], sconv, [], [ucv_b], ucv_b)
        conv_group(TS, None, s_conv_init)
        out_dma_ops.append(dma(ucv_s, ucv[:, :, TS:TS + 2], [ucv_b], [], ucv_b))
        chk(11)
        for q in range(4):
            def init_prev(q=q):
                for c, off in enumerate([q, 4 + q, 8 + q, 12, 13]):
                    dma(zr[:, c, 0:1], sshift[:, off:off + 1], [], [zr_b], zr_b)

            def H_init(q=q):
                dma(Hf4[:, q, :], swkv[:, q, :], [], [Hf4_b[q]], Hf4_b[q])
                cp(Hbf4[:, q, :], Hf4[:, q, :], [Hf4_b[q]], [Hbf4_b[q]])
            rwkv_pair(TS, init_prev=init_prev, H_init=H_init, y_dst=ysm[:, q, :], y_dst_b=ysm_b, first=True, last_out=True,
                      **pair_args(q))
            out_dma_ops.append(dma(H_s[:, q, :], Hf4[:, q, :], [Hf4_b[q]], [], Hf4_b[q]))
            out_dma_ops.append(dma(zl_s[:, q, :], zr[:, :, TS:TS + 1].rearrange("p c o -> p (c o)"), [zr_b], [], zr_b))
            chk(12 + q)

        def s_mix(c):
            if c < 4:
                return ycv[:, c, 0:TS], ycv_b
            return ysm[:, c - 4, :], ysm_b
        wout_tile(TS, s_mix, xs_in[:, :], SEG)


        norm_transpose(lambda ti: xhalo[:, :], 1, 2, hT, hT_b)

        def p_conv_init():
            for cq in range(4):
                pzh, pzh_b = next_z()
                for kc in range(8):
                    mm(pzh[:, 0:2], wcv[:, kc, (8 + cq) * 128:(9 + cq) * 128], hT[:, kc, 0:2], [wcv_kb[kc], wcv_kb2[kc], hT_b], [pzh_b],
                       start=(kc == 0), stop=(kc == 7))
                cp(zhs[:, 0:2], pzh[:, 0:2], [pzh_b], [zhs_b], eng="act")
                pzc, pzc_b = next_z()
                for kc in range(8):
                    mm(pzc[:, 0:2], wcv[:, kc, (4 + cq) * 128:(5 + cq) * 128], hT[:, kc, 0:2], [wcv_kb[kc], wcv_kb2[kc], hT_b], [pzc_b],
                       start=(kc == 0), stop=(kc == 7))
                tt(ucv[:, cq, 0:2], pzc[:, 0:2], zhs[:, 0:2], ALU.mult, [pzc_b, zhs_b], [ucv_b])
        p_conv_init()
        for g in range(SEG // 512):
            norm_transpose(lambda ti, g=g: xseg[g * 512 + ti * 128:g * 512 + (ti + 1) * 128, :], 4, 128, hT, hT_b)
            conv_group(512, None, (lambda: None) if g == 0 else None)
            for ti in range(4):
                t0 = g * 512 + ti * 128

                def p_mix(c, ti=ti, t0=t0):
                    if c < 4:
                        return ycv[:, c, ti * 128:(ti + 1) * 128], ycv_b
                    return ygt[:, c - 4, t0:t0 + 128], ygt_b
                wout_tile(128, p_mix, xseg[t0:t0 + 128, :], t0)
        out_dma_ops.append(dma(ucv_p, ucv[:, :, 512:514], [ucv_b], [], ucv_b))

        chk(4)
        P.barrier()
        stacks.pop("S").close()
        stacks.pop("A").close()
        esB = ExitStack()
        stacks["B"] = esB
        w1, w1_b = sb(esB, "w1", [128, 8, DFF], BF16)
        w2, w2_b = sb(esB, "w2", [128, 32, D], BF16)
        stgB = [sb(esB, f"stgB{i}", [128, 1024]) for i in range(2)]
        aT, aT_b = sb(esB, "aT", [128, 32, 256], BF16)
        x1g = [sb(esB, f"x1g{i}", [128, D]) for i in range(2)]
        yo = [sb(esB, f"yo{i}", [128, D]) for i in range(1)]
        stgB4 = stgB + [yo[0], x1g[1]]
        srot[0] = 0
        w1_kb = [[Buf(f"w1_{k}_{h}") for h in range(4)] for k in range(8)]
        w2_fb = [Buf(f"w2_{f}") for f in range(32)]
        for hf in range(4):
            for kc in range(8):
                load_cast(w1[:, kc, hf * 1024:(hf + 1) * 1024], w1_kb[kc][hf], w_ff1[kc * 128:(kc + 1) * 128, hf * 1024:(hf + 1) * 1024],
                          1024, gpre_t[:, 8 + kc:9 + kc], gpre_b, stg_=stgB4)
        for fc in range(32):
            load_cast(w2[:, fc, :], w2_fb[fc], w_ff2[fc * 128:(fc + 1) * 128, :], 1024, stg_=stgB4)

        def ffn_group(row0, ntiles, tn, ydst):
            T = (ntiles - 1) * 128 + tn
            for ti in range(ntiles):
                x_t, x_b = x1g[ti]
                dma(x_t[:tn, :], x1scr[row0 + ti * 128:row0 + ti * 128 + tn, :], [x1scr_b], [x_b], x_b)
                act(junk[:tn, :], x_t[:tn, :], AF.Square, [x_b], [junk_b, sm_b], accum=sm[:tn, 0:1])
                r = rstd_from(sm[:tn, 0:1], tn, 1.0 / D, NEPS, sm_b)
                act(xn[:tn, :], x_t[:tn, :], AF.Copy, [x_b, sm_b], [xn_b], scale=r)
                for kc in range(8):
                    tr(ps_tr[:, kc * 128:kc * 128 + tn], xn[:tn, kc * 128:(kc + 1) * 128], cmb[:tn, :tn], [xn_b, cmb_b], [ps_tr_b])
                src = ps_tr[:, :].rearrange("p (k t) -> p k t", k=8)[:, :, 0:tn]
                cp(hT[:, :, ti * 128:ti * 128 + tn], src, [ps_tr_b], [hT_b], eng=("act" if ti % 2 else "dve"))
            for fc in range(32):
                pz, pz_b = next_z()
                for kc in range(8):
                    mm(pz[:, 0:T], w1[:, kc, fc * 128:(fc + 1) * 128], hT[:, kc, 0:T], [w1_kb[kc][fc // 8], hT_b], [pz_b],
                       start=(kc == 0), stop=(kc == 7))
                if fc % 2 == 0:
                    ts(t1k[:, 0:T], pz[:, 0:T], 0.0, None, ALU.max, None, [pz_b], [t1k_b])
                    tt(aT[:, fc, 0:T], t1k[:, 0:T], t1k[:, 0:T], ALU.mult, [t1k_b], [aT_b])
                else:
                    act(x1t[:, 0:T], pz[:, 0:T], AF.Relu, [pz_b], [x1t_b])
                    tt(aT[:, fc, 0:T], x1t[:, 0:T], x1t[:, 0:T], ALU.mult, [x1t_b], [aT_b], eng="pool")
            for ti in range(ntiles):
                x_t, x_b = x1g[ti]
                bM = [ps_c[2 * (ti % 2)], ps_c[2 * (ti % 2) + 1]]
                bMb = [ps_c_b[2 * (ti % 2)], ps_c_b[2 * (ti % 2) + 1]]
                for half in range(2):
                    for fc in range(32):
                        mm(bM[half][:tn, :], aT[:, fc, ti * 128:ti * 128 + tn], w2[:, fc, half * 512:(half + 1) * 512],
                           [aT_b, w2_fb[fc]], bMb[half], start=(fc == 0), stop=(fc == 31))
                for half in range(2):
                    act(junk[:tn, 0:512], bM[half][:tn, :], AF.Square, bMb[half], [junk_b, sm2_b], accum=sm2[:tn, half:half + 1])
                tt(sm2[:tn, 2:3], sm2[:tn, 0:1], sm2[:tn, 1:2], ALU.add, [sm2_b], [sm2_b])
                r = rstd_from(sm2[:tn, 2:3], tn, 1.0 / D, NEPS, sm2_b)
                y_t, y_b = yo[0]
                for half in range(2):
                    hs = slice(half * 512, (half + 1) * 512)
                    stt(y_t[:tn, hs], bM[half][:tn, :], r, gffn[:tn, hs], ALU.mult, ALU.mult, bMb[half] + [sm_b, gffn_b], [y_b])
                tt(y_t[:tn, :], y_t[:tn, :], x_t[:tn, :], ALU.add, [y_b, x_b], [y_b])
                out_dma_ops.append(dma(ydst[ti * 128:ti * 128 + tn, :] if ydst is y_s else ydst[row0 + ti * 128:row0 + ti * 128 + tn, :],
                                       y_t[:tn, :], [y_b], [], y_b))

        for g in range(SEG // 256):
            ffn_group(g * 256, 2, 128, y_seg)
        ffn_group(SEG, 1, TS, y_s)


    import os
    KSTOP = int(os.environ.get("KDBG_STOP", "99"))

    def chk(n):
        if KSTOP == n:
            raise Stop()
    stacks = {"A": esA, "S": esS}
    try:
        _phases(chk, stacks)
    except Stop:
        pass
    P.finalize(out_dma_ops)
    with nc.allow_low_precision(reason="bf16 matmul operands by design"), \
            nc.allow_non_contiguous_dma(reason="tiny state vectors"), nc.Block() as block:
        P.emit(block)
    for k in ("B", "P", "S", "A"):
        if k in stacks:
            stacks[k].close()
    es.close()
    return nc


_NC = None


def _masks():
    C = 128
    i = np.arange(C)
    m = np.zeros((128, 7, 128), np.float32)
    m[:, 0, :] = np.eye(C)
    m[:, 1, :] = (i[:, None] < i[None, :])
    m[:, 2, :] = (i[:, None] <= i[None, :])
    m[:, 3, :] = -(i[:, None] < i[None, :]).astype(np.float32)
    m[:, 4, :] = (i[:, None] > i[None, :])
    bo = np.zeros((128, 128), np.float32)
    bo[:64, :64] = 1
    bo[64:, 64:] = 1
    m[:, 5, :] = bo
    m[:, 6, :] = 1.0
    return m


def kernel(x_prompt, x_sample, state_conv, state_shift, state_wkv, norm_mix_pre, norm_mix_post, norm_ffn_pre,
           norm_ffn_post, w_in, conv_w, shift_mu, w_decay2, decay_w0, w_a2, a0, w_g2, k_k, k_a, r_k, gn_gain,
           gn_bias, w_out, w_ff1, w_ff2):
    global _NC
    f = lambda a: np.ascontiguousarray(np.asarray(a, dtype=np.float32))
    x_prompt, x_sample = f(x_prompt), f(x_sample)
    w_in0, w_out0, w_ff10, w_ff20 = f(w_in)[0], f(w_out)[0], f(w_ff1)[0], f(w_ff2)[0]
    mu = f(shift_mu)[0]
    fm = lambda v: np.ascontiguousarray(v.reshape(-1, 128).T)
    wda_full = np.concatenate([f(w_decay2)[0], f(w_a2)[0]], axis=0)
    wg_full = f(w_g2)[0]
    plist = [mu[0:512], mu[512:1024], mu[1024:1536], f(decay_w0)[0], f(a0)[0], f(k_k)[0], f(k_a)[0],
             f(r_k)[0].reshape(-1), f(gn_gain)[0], f(gn_bias)[0]]
    par_full = np.ascontiguousarray(np.stack([fm(p) for p in plist], axis=-1))
    mu_sh = np.ascontiguousarray(np.stack([mu[1536:1664], mu[1664:1792]], axis=-1))
    convw = np.ascontiguousarray(f(conv_w)[0].T.reshape(4, 128, 3).transpose(1, 0, 2))
    gpre = np.ascontiguousarray(np.concatenate([fm(f(norm_mix_pre)[0]), fm(f(norm_ffn_pre)[0])], axis=1))
    gpost = np.ascontiguousarray(np.broadcast_to(f(norm_mix_post)[0][None, :], (128, D)))
    gffn = np.ascontiguousarray(np.broadcast_to(f(norm_ffn_post)[0][None, :], (128, D)))
    cmask = _masks()
    sc, ssh, swk = f(state_conv)[0], f(state_shift)[0], f(state_wkv)[0]
    in_maps = []
    for c in range(8):
        b, j = c // 4, c % 4
        xh = x_prompt[b, SEG * j - 2:SEG * j] if j > 0 else np.zeros((2, D), np.float32)
        xp = np.zeros((3 * SEG, D), np.float32)
        if j > 0:
            xp[(3 - j) * SEG:] = x_prompt[b, 0:SEG * j]
        in_maps.append({
            "xpre": xp, "xseg": np.ascontiguousarray(x_prompt[b, SEG * j:SEG * (j + 1)]),
            "xhalo": np.ascontiguousarray(xh), "xs": x_sample[c],
            "sconv": np.ascontiguousarray(sc[c].T.reshape(4, 128, 2).transpose(1, 0, 2)),
            "sshift": fm(ssh[c]),
            "swkv": np.ascontiguousarray(swk[c].reshape(4, 2, 64, 64).transpose(1, 3, 0, 2).reshape(128, 4, 64)),
            "w_in": w_in0, "w_out": w_out0, "w_ff1": w_ff10, "w_ff2": w_ff20,
            "wda_full": wda_full, "wg_full": wg_full,
            "par_full": par_full, "mu_sh": mu_sh,
            "convw": convw, "gpre": gpre, "gpost": gpost, "gffn": gffn, "cmask": cmask,
        })
    if _NC is None:
        _NC = build_program()
    res = run_bass_kernel_spmd(_NC, in_maps, core_ids=list(range(8))).results
    y_prompt = np.zeros((2, SEQ, D), np.float32)
    y_sample = np.zeros((8, TS, D), np.float32)
    conv_p = np.zeros((1, 2, 2, 512), np.float32)
    shift_p = np.zeros((1, 2, 1792), np.float32)
    wkv_p = np.zeros((1, 2, 8, 64, 64), np.float32)
    conv_s = np.zeros((1, 8, 2, 512), np.float32)
    shift_s = np.zeros((1, 8, 1792), np.float32)
    wkv_s = np.zeros((1, 8, 8, 64, 64), np.float32)
    for c in range(8):
        r = res[c]
        b, j = c // 4, c % 4
        y_prompt[b, SEG * j:SEG * (j + 1)] = r["y_seg"]
        y_sample[c] = r["y_s"]
        if j == 3:
            conv_p[0, b] = r["ucv_p"].transpose(2, 1, 0).reshape(2, 512)
        conv_s[0, c] = r["ucv_s"].transpose(2, 1, 0).reshape(2, 512)
        if j == 3:
            zl = r["zl_p"]
            for q in range(4):
                for k in range(3):
                    shift_p[0, b, k * 512 + q * 128:k * 512 + (q + 1) * 128] = zl[:, q, k]
            shift_p[0, b, 1536:1664] = zl[:, 0, 3]
            shift_p[0, b, 1664:1792] = zl[:, 0, 4]
            wkv_p[0, b] = r["H_p"].reshape(2, 64, 4, 64).transpose(2, 0, 3, 1).reshape(8, 64, 64)
        zs_ = r["zl_s"]
        for q in range(4):
            for k in range(3):
                shift_s[0, c, k * 512 + q * 128:k * 512 + (q + 1) * 128] = zs_[:, q, k]
        shift_s[0, c, 1536:1664] = zs_[:, 0, 3]
        shift_s[0, c, 1664:1792] = zs_[:, 0, 4]
        wkv_s[0, c] = r["H_s"].reshape(2, 64, 4, 64).transpose(2, 0, 3, 1).reshape(8, 64, 64)
    return (y_prompt, y_sample, conv_p, shift_p, wkv_p, conv_s, shift_s, wkv_s)
```
